# Optimizing a Trainium2 kernel written in Bass

```python
import jax, jax.numpy as jnp
from jax import lax
import numpy as np

D_MODEL = 4096
BATCH = 8
SEQ = 2048
DEPTH = 2
DEC_BATCH = 1
DEC_SEQ = 16384
PAST_LEN = 128

HEAD_DIM = 128
GLA_WIDTH = 3 * D_MODEL // 8
GLA_DV = 2 * HEAD_DIM
GLA_HEADS = GLA_WIDTH // GLA_DV
GLA_DK = HEAD_DIM
GLA_QK_WIDTH = GLA_HEADS * GLA_DK
GLA_CHUNK = 64
GLA_GATE_RANK = 16
GLA_GATE_TAU = 16.0
SGU_WIDTH = D_MODEL // 4
SGU_CHUNK = 128
SGU_GROUP_DIM = 128
SGU_GROUPS = SGU_WIDTH // SGU_GROUP_DIM
ATT_WIDTH = 3 * D_MODEL // 8
ATT_Q_HEADS = ATT_WIDTH // HEAD_DIM
ATT_KV_HEADS = 4
ATT_GROUP = ATT_Q_HEADS // ATT_KV_HEADS
ATT_KV_WIDTH = ATT_KV_HEADS * HEAD_DIM
WINDOW = 128
ROPE_THETA = 500000.0
ROPE_DIMS = HEAD_DIM // 4
N_BRANCH = 3
MIX_WIDTH = GLA_WIDTH + SGU_WIDTH + ATT_WIDTH
SPLIT_SIZES = (GLA_QK_WIDTH, GLA_QK_WIDTH, GLA_WIDTH, 2 * GLA_GATE_RANK, GLA_WIDTH,
               SGU_WIDTH, SGU_WIDTH, SGU_WIDTH,
               ATT_WIDTH, ATT_KV_WIDTH, ATT_KV_WIDTH, ATT_WIDTH,
               N_BRANCH * D_MODEL)
N_IN = sum(SPLIT_SIZES)
NORM_EPS = 1e-6
LN_EPS = 1e-5

kernel_name = "hybrid_gla_sgu_swa_gated_encoder"


def _rms_norm(x, gain):
    xf = x.astype(jnp.float32)
    y = xf * lax.rsqrt(jnp.mean(xf * xf, axis=-1, keepdims=True) + NORM_EPS)
    return (y * gain.astype(jnp.float32)).astype(x.dtype)


def _layer_norm(x, gain, bias):
    xf = x.astype(jnp.float32)
    mu = jnp.mean(xf, axis=-1, keepdims=True)
    xc = xf - mu
    y = xc * lax.rsqrt(jnp.mean(xc * xc, axis=-1, keepdims=True) + LN_EPS)
    return (y * gain.astype(jnp.float32) + bias.astype(jnp.float32)).astype(x.dtype)


def _partial_rope(x):
    S = x.shape[1]
    inv_freq = ROPE_THETA ** (-jnp.arange(0, ROPE_DIMS, 2, dtype=jnp.float32) / ROPE_DIMS)
    ang = jnp.arange(S, dtype=jnp.float32)[:, None] * inv_freq[None, :]
    cos = jnp.cos(ang)[None, :, None, :]
    sin = jnp.sin(ang)[None, :, None, :]
    xf = x.astype(jnp.float32)
    half = ROPE_DIMS // 2
    x1 = xf[..., :half]
    x2 = xf[..., half:ROPE_DIMS]
    out = jnp.concatenate([x1 * cos - x2 * sin, x2 * cos + x1 * sin, xf[..., ROPE_DIMS:]], axis=-1)
    return out.astype(x.dtype)


def _gla_chunked(q, k, v, log_a):
    B, S, H, DK = q.shape
    DV = v.shape[-1]
    C = GLA_CHUNK
    NC = S // C

    def to_chunks(t):
        return t.astype(jnp.float32).reshape(B, NC, C, H, t.shape[-1]).transpose(1, 0, 3, 2, 4)

    qc = to_chunks(q) * (DK ** -0.5)
    kc = to_chunks(k)
    vc = to_chunks(v)
    gc = to_chunks(log_a)
    lower = jnp.tril(jnp.ones((C, C), dtype=bool))

    def step(state, inp):
        qi, ki, vi, gi = inp
        b = jnp.cumsum(gi, axis=2)
        b_last = b[:, :, -1:, :]
        o_inter = jnp.einsum('bhtd,bhde->bhte', qi * jnp.exp(b), state)
        diff = b[:, :, :, None, :] - b[:, :, None, :, :]
        decay = jnp.exp(jnp.where(lower[:, :, None], diff, -jnp.inf))
        scores = jnp.einsum('bhtsd,bhsd->bhts', qi[:, :, :, None, :] * decay, ki)
        o_intra = jnp.einsum('bhts,bhse->bhte', scores, vi)
        new_state = (jnp.exp(b_last[:, :, 0, :, None]) * state
                     + jnp.einsum('bhsd,bhse->bhde', ki * jnp.exp(b_last - b), vi))
        return new_state, o_inter + o_intra

    state0 = jnp.zeros((B, H, DK, DV), jnp.float32)
    _, o = lax.scan(step, state0, (qc, kc, vc, gc))
    return o.transpose(1, 0, 3, 2, 4).reshape(B, S, H, DV)


def _window_attention(q, k, v, sink):
    B, S = q.shape[:2]
    nb = S // WINDOW
    qb = q.reshape(B, nb, WINDOW, ATT_KV_HEADS, ATT_GROUP, HEAD_DIM)
    pad = ((0, 0), (WINDOW, WINDOW), (0, 0), (0, 0))
    kp = jnp.pad(k, pad).reshape(B, nb + 2, WINDOW, ATT_KV_HEADS, HEAD_DIM)
    vp = jnp.pad(v, pad).reshape(B, nb + 2, WINDOW, ATT_KV_HEADS, HEAD_DIM)
    kb = jnp.concatenate([kp[:, :-2], kp[:, 1:-1], kp[:, 2:]], axis=2)
    vb = jnp.concatenate([vp[:, :-2], vp[:, 1:-1], vp[:, 2:]], axis=2)
    s = jnp.einsum('bnqhgd,bnkhd->bnhgqk', qb, kb,
                   preferred_element_type=jnp.float32) * (HEAD_DIM ** -0.5)
    blk = jnp.arange(nb)[:, None, None]
    qpos = blk * WINDOW + jnp.arange(WINDOW)[None, :, None]
    kpos = blk * WINDOW - WINDOW + jnp.arange(3 * WINDOW)[None, None, :]
    valid = (jnp.abs(kpos - qpos) <= WINDOW) & (kpos >= 0) & (kpos < S)
    s = jnp.where(valid[None, :, None, None], s, -jnp.inf)
    sink_logit = jnp.broadcast_to(
        sink.astype(jnp.float32).reshape(1, 1, ATT_KV_HEADS, ATT_GROUP, 1, 1), s.shape[:-1] + (1,))
    p = jax.nn.softmax(jnp.concatenate([s, sink_logit], axis=-1), axis=-1)[..., :-1]
    o = jnp.einsum('bnhgqk,bnkhd->bnqhgd', p.astype(v.dtype), vb)
    return o.reshape(B, S, ATT_Q_HEADS * HEAD_DIM)


def _layer(x, norm_gain, w_in, gla_gate_up, gla_gate_bias, gla_norm_gain,
           sgu_ln_gain, sgu_ln_bias, sgu_w, sgu_b, q_norm_gain, k_norm_gain, sink,
           gate_bias, w_br, w_out):
    B, S, _ = x.shape
    h = _rms_norm(x, norm_gain) @ w_in
    split_idx = [int(i) for i in np.cumsum(SPLIT_SIZES)[:-1]]
    (a_q, a_k, a_v, a_lr, a_gate, b_u, b_v, b_gate,
     c_q, c_k, c_v, c_gate, g_merge) = jnp.split(h, split_idx, axis=-1)

    qa = a_q.reshape(B, S, GLA_HEADS, GLA_DK)
    ka = a_k.reshape(B, S, GLA_HEADS, GLA_DK)
    va = a_v.reshape(B, S, GLA_HEADS, GLA_DV)
    lr = a_lr.reshape(B, S, 2, GLA_GATE_RANK)
    z = jnp.einsum('bsjr,jrk->bsjk', lr, gla_gate_up) + gla_gate_bias
    log_a = (jax.nn.log_sigmoid(z.astype(jnp.float32)) / GLA_GATE_TAU).reshape(B, S, 2, GLA_HEADS, GLA_DK)
    fwd = _gla_chunked(qa, ka, va, log_a[:, :, 0])
    rev = lambda t: jnp.flip(t, axis=1)
    bwd = rev(_gla_chunked(rev(qa), rev(ka), rev(va), rev(log_a[:, :, 1])))
    o_a = _rms_norm(fwd + bwd, gla_norm_gain).astype(x.dtype).reshape(B, S, GLA_WIDTH) * jax.nn.silu(a_gate)

    u = jax.nn.gelu(b_u)
    vv = _layer_norm(jax.nn.gelu(b_v), sgu_ln_gain, sgu_ln_bias)
    vr = vv.reshape(B, S // SGU_CHUNK, SGU_CHUNK, SGU_GROUPS, SGU_GROUP_DIM)
    mixed = jnp.einsum('gpq,bnqgc->bnpgc', sgu_w, vr) + sgu_b.T[None, None, :, :, None]
    o_b = u * mixed.reshape(B, S, SGU_WIDTH) * jax.nn.silu(b_gate)

    qc = _partial_rope(_rms_norm(c_q.reshape(B, S, ATT_Q_HEADS, HEAD_DIM), q_norm_gain))
    kc = _partial_rope(_rms_norm(c_k.reshape(B, S, ATT_KV_HEADS, HEAD_DIM), k_norm_gain))
    vc = c_v.reshape(B, S, ATT_KV_HEADS, HEAD_DIM)
    o_c = _window_attention(qc, kc, vc, sink) * jax.nn.silu(c_gate)

    gates = jax.nn.sigmoid(g_merge.reshape(B, S, N_BRANCH, D_MODEL) + gate_bias)
    y_a = o_a @ w_br[:GLA_WIDTH]
    y_b = o_b @ w_br[GLA_WIDTH:GLA_WIDTH + SGU_WIDTH]
    y_c = o_c @ w_br[GLA_WIDTH + SGU_WIDTH:]
    merged = gates[:, :, 0] * y_a + gates[:, :, 1] * y_b + gates[:, :, 2] * y_c
    return x + merged @ w_out


def setup_inputs(seed: int = 0) -> dict:
    key = jax.random.key(seed)
    ks = jax.random.split(key, 17)
    nrm = jax.random.normal
    f32 = jnp.float32
    return {
        "x_prompt": nrm(ks[0], (BATCH, SEQ, D_MODEL), f32),
        "x_sample": nrm(ks[1], (DEC_BATCH, DEC_SEQ, D_MODEL), f32),
        "norm_gain": 1.0 + 0.02 * nrm(ks[2], (DEPTH, D_MODEL), f32),
        "w_in": nrm(ks[3], (DEPTH, D_MODEL, N_IN), f32) * (D_MODEL ** -0.5),
        "gla_gate_up": nrm(ks[4], (DEPTH, 2, GLA_GATE_RANK, GLA_QK_WIDTH), f32) * (GLA_GATE_RANK ** -0.5),
        "gla_gate_bias": 0.1 * nrm(ks[5], (DEPTH, 2, GLA_QK_WIDTH), f32),
        "gla_norm_gain": 1.0 + 0.02 * nrm(ks[6], (DEPTH, GLA_DV), f32),
        "sgu_ln_gain": 1.0 + 0.02 * nrm(ks[7], (DEPTH, SGU_WIDTH), f32),
        "sgu_ln_bias": 0.02 * nrm(ks[8], (DEPTH, SGU_WIDTH), f32),
        "sgu_w": nrm(ks[9], (DEPTH, SGU_GROUPS, SGU_CHUNK, SGU_CHUNK), f32) * (SGU_CHUNK ** -0.5),
        "sgu_b": 1.0 + 0.02 * nrm(ks[10], (DEPTH, SGU_GROUPS, SGU_CHUNK), f32),
        "q_norm_gain": 1.0 + 0.02 * nrm(ks[11], (DEPTH, HEAD_DIM), f32),
        "k_norm_gain": 1.0 + 0.02 * nrm(ks[12], (DEPTH, HEAD_DIM), f32),
        "sink": 0.5 * nrm(ks[13], (DEPTH, ATT_Q_HEADS), f32),
        "gate_bias": 0.1 * nrm(ks[14], (DEPTH, N_BRANCH, D_MODEL), f32),
        "w_br": nrm(ks[15], (DEPTH, MIX_WIDTH, D_MODEL), f32) * (MIX_WIDTH ** -0.5),
        "w_out": nrm(ks[16], (DEPTH, D_MODEL, D_MODEL), f32) * (D_MODEL ** -0.5),
    }


def reference(x_prompt, x_sample, norm_gain, w_in, gla_gate_up, gla_gate_bias, gla_norm_gain,
              sgu_ln_gain, sgu_ln_bias, sgu_w, sgu_b, q_norm_gain, k_norm_gain, sink,
              gate_bias, w_br, w_out):
    y_prompt = x_prompt
    y_sample = x_sample
    for l in range(DEPTH):
        params = (norm_gain[l], w_in[l], gla_gate_up[l], gla_gate_bias[l], gla_norm_gain[l],
                  sgu_ln_gain[l], sgu_ln_bias[l], sgu_w[l], sgu_b[l], q_norm_gain[l],
                  k_norm_gain[l], sink[l], gate_bias[l], w_br[l], w_out[l])
        y_prompt = _layer(y_prompt, *params)
        y_sample = _layer(y_sample, *params)
    return (y_prompt, y_sample)
```

```python
import contextlib
import numpy as np
import concourse.bass as bass
import concourse.mybir as mybir
from concourse.bass_utils import run_bass_kernel_spmd

F32 = mybir.dt.float32
BF16 = mybir.dt.bfloat16
AF = mybir.ActivationFunctionType
ALU = mybir.AluOpType
AX = mybir.AxisListType

D = 4096
NIN = 24096
HALO = 256
ALL_SP = True
ROPE_THETA = 500000.0

SPL = dict(a_q=(0, 768), a_k=(768, 768), a_v=(1536, 1536), a_lr=(3072, 32), a_gate=(3104, 1536),
           b_u=(4640, 1024), b_v=(5664, 1024), b_gate=(6688, 1024), c_q=(7712, 1536), c_k=(9248, 512),
           c_v=(9760, 512), c_gate=(10272, 1536), g=(11808, 12288))
FM_GROUPS = ["a_q", "a_k", "a_lr", "a_gate", "b_u", "b_gate", "c_q", "c_k", "c_gate", "g"]
TM_GROUPS = ["a_k", "a_v", "b_v", "c_v"]
FM_OFF = dict(a_q=0, a_k=768, a_gate=1536, b_u=3072, b_gate=4096, c_q=5120, c_k=6656, c_gate=7168, g=8704)
FM_ROWS = 8704 + 12288
TM_OFF = dict(a_k=0, a_v=768, b_v=2304, c_v=3328)
TM_COLS = 3840
TMW = 256


def fm_blocks():
    out = []
    for g in FM_GROUPS:
        s, w = SPL[g]
        nb = max(1, w // 128)
        for b in range(nb):
            out.append((g, b, s + b * 128, min(128, w)))
    return out


def tm_blocks():
    out = []
    for g in TM_GROUPS:
        s, w = SPL[g]
        for b in range(w // TMW):
            out.append((g, b, s + b * TMW, TMW))
    return out


FMB = fm_blocks()
TMB = tm_blocks()
NFMB = len(FMB) + 32
NTMB = len(TMB) + 16


class _Stop(Exception):
    pass


class Buf:
    __slots__ = ("name", "w", "r", "sem", "semval", "queue", "rd")

    def __init__(self, name):
        self.name = name
        self.w = None
        self.r = {}
        self.sem = None
        self.semval = 0
        self.queue = None
        self.rd = None


class Ring:
    def __init__(self, slots, load_fn, ntasks):
        self.slots = slots
        self.load_fn = load_fn
        self.n = ntasks
        self.next = 0
        for _ in range(len(slots)):
            self.issue()

    def issue(self):
        if self.next < self.n:
            i = self.next
            t, b = self.slots[i % len(self.slots)]
            self.load_fn(i, t, b)
            self.next += 1

    def get(self, i):
        return self.slots[i % len(self.slots)]

    def done(self, i):
        self.issue()


class K:
    EPOCH = 40000
    CE = ("pe", "act", "dve", "pool")

    def __init__(self, nc, es):
        self.nc = nc
        self.es = es
        self.e = dict(pe=nc.tensor, act=nc.scalar, dve=nc.vector, pool=nc.gpsimd, sp=nc.sync)
        self.seq = {e: 0 for e in self.CE}
        self.sems = {e: [] for e in self.CE}
        self.known = {e: {} for e in self.e}
        self.dmabufs = []
        self.allbufs = []
        self.free = []
        self.nsem = 0
        self.ninst = 0
        self.round = 0
        self.ntile = 0
        self.bar_a = es.enter_context(nc.semaphore("bar_a"))
        self.bar_b = es.enter_context(nc.semaphore("bar_b"))

    def buf(self, name):
        b = Buf(name)
        self.allbufs.append(b)
        return b

    def _sem(self, name):
        if self.free:
            return self.free.pop()
        self.nsem += 1
        return self.es.enter_context(self.nc.semaphore("s%d" % self.nsem))

    def esem(self, e, n):
        ep = (n - 1) // self.EPOCH
        while len(self.sems[e]) <= ep:
            self.sems[e].append(self._sem("s_%s_%d" % (e, len(self.sems[e]))))
        return self.sems[e][ep], n - ep * self.EPOCH

    def _wait(self, waiter, dep):
        if dep is None:
            return
        kn = self.known[waiter]
        if dep[0] == "e":
            _, f, n = dep
            if f == waiter and f == "pe":
                return
            if kn.get(f, 0) >= n:
                return
            sem, val = self.esem(f, n)
            self.e[waiter].wait_ge(sem, val)
            kn[f] = n
        else:
            _, buf, val = dep
            key = ("d", id(buf))
            if kn.get(key, 0) >= val:
                return
            self.e[waiter].wait_ge(buf.sem, val)
            kn[key] = val
        self.ninst += 1

    def _deps(self, r, w):
        deps = []
        for b in r:
            deps.append(b.w)
        for b in w:
            deps.append(b.w)
            deps.append(b.rd)
            for f, n in b.r.items():
                deps.append(("e", f, n))
        return deps

    def I(self, eng, fn, r=(), w=()):
        for d in self._deps(r, w):
            self._wait(eng, d)
        ins = fn()
        self.seq[eng] += 1
        n = self.seq[eng]
        sem, _ = self.esem(eng, n)
        ins.then_inc(sem, 1)
        self.ninst += 1
        for b in r:
            if b.r.get(eng, 0) < n:
                b.r[eng] = n
        for b in w:
            b.w = ("e", eng, n)
            b.r = {}
            b.rd = None
        return ins

    def dma(self, q, out, in_, r=(), w=(), **kw):
        if ALL_SP:
            q = "sp"
        for d in self._deps(r, w):
            self._wait(q, d)
        tr = list(r) + list(w)
        assert len(tr) == 1, "one tracked SBUF buf per DMA"
        b = tr[0]
        if b.sem is None:
            b.sem = self._sem("sd_" + b.name)
            b.queue = q
            b.semval = 0
            self.dmabufs.append(b)
        assert b.queue == q, (b.name, b.queue, q)
        ins = self.e[q].dma_start(out=out, in_=in_, **kw)
        b.semval += 16
        assert b.semval < 65000, b.name
        ins.then_inc(b.sem, 16)
        self.ninst += 1
        if w:
            b.w = ("d", b, b.semval)
            b.r = {}
            b.rd = None
        else:
            b.rd = ("d", b, b.semval)
        return ins

    def barrier(self):
        for e in self.e:
            for f in self.CE:
                if f != e and self.seq[f] > 0:
                    self._wait(e, ("e", f, self.seq[f]))
            for b in self.dmabufs:
                if b.semval > 0:
                    self._wait(e, ("d", b, b.semval))

    def reset(self):
        self.barrier()
        self.round += 1
        pool = self.e["pool"]
        if self.seq["pool"] > 0:
            sem, val = self.esem("pool", self.seq["pool"])
            pool.wait_ge(sem, val)
        for e in self.e:
            self.e[e].sem_inc(self.bar_a, 1)
        pool.wait_ge(self.bar_a, 5 * self.round)
        used = []
        for e in self.CE:
            used += self.sems[e]
            self.sems[e] = []
        for b in self.dmabufs:
            used.append(b.sem)
            b.sem = None
            b.semval = 0
            b.queue = None
        self.dmabufs = []
        for sm in used:
            pool.sem_clear(sm)
        pool.sem_inc(self.bar_b, 1)
        for e in self.e:
            self.e[e].wait_ge(self.bar_b, self.round)
        self.free += used
        for b in self.allbufs:
            b.w = None
            b.r = {}
            b.rd = None
        self.seq = {e: 0 for e in self.CE}
        self.known = {e: {} for e in self.e}
        self.ninst += 20 + len(used)


def build_program(cfg):
    nc = bass.Bass("TRN2", target_bir_lowering=False)
    depth = cfg["depth"]
    NTM = max(L["NT"] for L in cfg["layers"])
    NBLK = NTM // 128

    def din(name, shape, dt=F32):
        return nc.dram_tensor(name, list(shape), dt, kind="ExternalInput").ap()

    x0 = din("x0", [cfg["layers"][0]["NT"], D])
    norm_gain = din("norm_gain", [depth, D])
    w_in = din("w_in", [depth, D, NIN])
    gate_up = din("gla_gate_up", [depth, 2, 16, 768])
    gate_b = din("gla_gate_bias", [depth, 2, 768])
    gla_ng = din("gla_norm_gain", [depth, 256])
    ln_g = din("sgu_ln_gain", [depth, 1024])
    ln_b = din("sgu_ln_bias", [depth, 1024])
    sgu_w = din("sgu_w", [depth, 8, 128, 128])
    sgu_b = din("sgu_b", [depth, 8, 128])
    qng = din("q_norm_gain", [depth, 128])
    kng = din("k_norm_gain", [depth, 128])
    sink = din("sink", [depth, 12])
    gbias = din("gate_bias", [depth, 3, D])
    w_br = din("w_br", [depth, D, D])
    w_out = din("w_out", [depth, D, D])
    cst = din("cst", [128, 6 * 128])
    ropec = din("ropec", [depth, 32, NTM])
    ropes = din("ropes", [depth, 32, NTM])
    kvalid = din("kvalid", [depth, 128, NBLK])
    youts = []
    for nm, n in cfg["outs"]:
        youts.append(nc.dram_tensor(nm, [n, D], F32, kind="ExternalOutput").ap())
    outmap = {nm: ap for (nm, _), ap in zip(cfg["outs"], youts)}

    dbg = cfg.get("debug", False)
    stop = cfg.get("stop", None)
    skind = "ExternalOutput" if dbg else "Internal"
    FMW = [nc.dram_tensor("FMW%d" % i, [NFMB, 128, 4096], BF16, kind=skind).ap() for i in range(depth)]
    TMWs = [nc.dram_tensor("TMWs%d" % i, [NTMB, 128, 32 * TMW], BF16, kind=skind).ap() for i in range(depth)]
    FM = nc.dram_tensor("FM", [FM_ROWS, NTM], BF16, kind=skind).ap()
    LRT = nc.dram_tensor("LRT", [32, NTM], F32, kind=skind).ap()
    TM = nc.dram_tensor("TM", [NTM, TM_COLS], BF16, kind=skind).ap()
    OFM = nc.dram_tensor("OFM", [4096, NTM], BF16, kind=skind).ap()
    OFW = nc.dram_tensor("OFW", [1536, NTM], F32, kind=skind).ap()
    X1 = nc.dram_tensor("X1", [NTM, D], F32, kind=skind).ap()
    outmap["x1"] = X1

    with contextlib.ExitStack() as es:
        k = K(nc, es)
        E = es.enter_context

        def sb(name, shape, dt=F32, stack=None):
            k.ntile += 1
            name = "%s_%d" % (name, k.ntile)
            t = (stack or es).enter_context(nc.sbuf_tensor(name, list(shape), dt))
            return t, k.buf(name)

        PS = E(nc.psum_tensor("PS", [128, 8 * 512], F32))
        psb = [k.buf("ps%d" % i) for i in range(8)]
        pstate = {"i": 0}

        def ps(n=1):
            i = pstate["i"]
            if i + n > 8:
                i = 0
            pstate["i"] = (i + n) % 8
            return PS[:, i * 512:(i + n) * 512], psb[i:i + n]

        cst_f, cst_fb = sb("cst_f", [128, 768])
        cst_h, cst_hb = sb("cst_h", [128, 768], BF16)
        ones_h, ones_hb = sb("ones_h", [128, 128], BF16)
        k.dma("sp", cst_f[:], cst, w=[cst_fb])
        k.I("dve", lambda: nc.vector.tensor_copy(out=cst_h[:], in_=cst_f[:]), r=[cst_fb], w=[cst_hb])
        k.I("pool", lambda: nc.gpsimd.memset(ones_h[:], 1.0), w=[ones_hb])
        epsq, epsqb = sb("epsq", [128, 4], F32)
        k.I("pool", lambda: nc.gpsimd.memset(epsq[:, 0:1], 128e-6), w=[epsqb])
        k.I("pool", lambda: nc.gpsimd.memset(epsq[:, 1:2], 1e-5), w=[epsqb])
        k.I("pool", lambda: nc.gpsimd.memset(epsq[:, 2:3], 1e-6), w=[epsqb])
        k.I("pool", lambda: nc.gpsimd.memset(epsq[:, 3:4], 1.0), w=[epsqb])
        ident_h = cst_h[:, 0:128]
        ident_f = cst_f[:, 0:128]
        U_f = cst_f[:, 128:256]
        L_f = cst_f[:, 256:384]
        Un_f = cst_f[:, 384:512]
        Ln_f = cst_f[:, 512:640]
        rot_h = cst_h[:, 640:768]

        with contextlib.ExitStack() as st:
            NS = 4
            stg = [sb("cv_s%d" % i, [128, 32, 256], F32, st) for i in range(NS)]
            cvo = [sb("cv_o%d" % i, [128, 32 * 256], BF16, st) for i in range(NS)]
            ceng = ["dve", "pool", "act"]
            ctasks = []
            for l in range(1):
                j = 0
                while j < len(FMB):
                    g, b, c0, wd = FMB[j]
                    if wd == 128 and j + 1 < len(FMB) and FMB[j + 1][0] == g:
                        src = w_in[l, :, c0:c0 + 256].rearrange("(k p) c -> p k c", p=128)
                        ctasks.append((src, 256, [(FMW[l][j].rearrange("p (k c) -> p k c", c=128), 0, 128),
                                                  (FMW[l][j + 1].rearrange("p (k c) -> p k c", c=128), 128, 128)]))
                        j += 2
                    else:
                        src = w_in[l, :, c0:c0 + wd].rearrange("(k p) c -> p k c", p=128)
                        ctasks.append((src, wd, [(FMW[l][j, :, 0:32 * wd].rearrange("p (k c) -> p k c", c=wd), 0, wd)]))
                        j += 1
                for f in range(0, 32, 2):
                    src = w_br[l, :, f * 128:(f + 2) * 128].rearrange("(k p) c -> p k c", p=128)
                    ctasks.append((src, 256, [(FMW[l][len(FMB) + f].rearrange("p (k c) -> p k c", c=128), 0, 128),
                                              (FMW[l][len(FMB) + f + 1].rearrange("p (k c) -> p k c", c=128), 128, 128)]))
                for j, (g, b, c0, wd) in enumerate(TMB):
                    src = w_in[l, :, c0:c0 + TMW].rearrange("(k p) c -> p k c", p=128)
                    ctasks.append((src, TMW, [(TMWs[l][j].rearrange("p (k c) -> p k c", c=TMW), 0, TMW)]))
                for f in range(16):
                    src = w_out[l, :, f * TMW:(f + 1) * TMW].rearrange("(k p) c -> p k c", p=128)
                    ctasks.append((src, TMW, [(TMWs[l][len(TMB) + f].rearrange("p (k c) -> p k c", c=TMW), 0, TMW)]))

            def cload(i, t, b):
                src, W, dsts = ctasks[i]
                k.dma("sp", t[:, :, 0:W], src, w=[b])

            cring = Ring(stg, cload, len(ctasks))
            ci = 0
            for i, (src, W, dsts) in enumerate(ctasks):
                s_, sbf = cring.get(i)
                o, obf = cvo[i % NS]
                off = 0
                views = []
                for (dst, c0, w) in dsts:
                    ov = o[:, off:off + 32 * w].rearrange("p (k c) -> p k c", c=w)
                    off += 32 * w
                    e = ceng[ci % 3]
                    ci += 1
                    if e == "act":
                        k.I("act", lambda: nc.scalar.copy(out=ov, in_=s_[:, :, c0:c0 + w]), r=[sbf], w=[obf])
                    elif e == "dve":
                        k.I("dve", lambda: nc.vector.tensor_copy(out=ov, in_=s_[:, :, c0:c0 + w]), r=[sbf], w=[obf])
                    else:
                        k.I("pool", lambda: nc.gpsimd.tensor_copy(out=ov, in_=s_[:, :, c0:c0 + w]), r=[sbf], w=[obf])
                    views.append((dst, ov))
                cring.done(i)
                for (dst, ov) in views:
                    k.dma("sp", dst, ov, r=[obf])
            k.reset()

        def chk(tag):
            return stop == tag

        def do_layer(l):
            L = cfg["layers"][l]
            NT = L["NT"]
            xsrc = x0 if L["xsrc"] == "x0" else X1
            with contextlib.ExitStack() as lst:
                gb_t, gb_tb = sb("gb_t", [128, 96], F32, lst)
                gb_r, gb_rb = sb("gb_r", [96, 128], F32, lst)
                k.dma("sp", gb_r[:], gbias[l].rearrange("i (b p) -> (i b) p", p=128), w=[gb_rb])
                pg_, pgb_ = ps(1)
                k.I("pe", lambda: nc.tensor.matmul(pg_[:, 0:96], lhsT=gb_r[:], rhs=cst_f[0:96, 0:96], start=True, stop=True),
                    r=[gb_rb, cst_fb], w=pgb_)
                k.I("dve", lambda: nc.vector.tensor_copy(out=gb_t[:], in_=pg_[:, 0:96]), r=pgb_, w=[gb_tb])
                qkg, qkgb = sb("qkg", [128, 2], F32, lst)
                k.dma("sp", qkg[:, 0:1], qng[l].rearrange("(p o) -> p o", o=1), w=[qkgb])
                k.dma("sp", qkg[:, 1:2], kng[l].rearrange("(p o) -> p o", o=1), w=[qkgb])
                k.I("dve", lambda: nc.vector.tensor_scalar_mul(out=qkg[:], in0=qkg[:], scalar1=float(np.sqrt(128.0))),
                    r=[qkgb], w=[qkgb])

                if chk("p1a"):

                    return True
                with contextlib.ExitStack() as st:
                    gain_bc, gain_bcb = sb("gain_bc", [128, D], F32, st)
                    k.dma("sp", gain_bc[:], norm_gain[l:l + 1, :].to_broadcast([128, D]), w=[gain_bcb])
                    xnT, xnTb = sb("xnT", [128, 32, 512], BF16, st)
                    xld = [sb("xld%d" % i, [128, D], F32, st) for i in range(1)]
                    xnb = [sb("xnb%d" % i, [128, D], BF16, st) for i in range(1)]
                    junk, junkb = sb("junk", [128, D], BF16, st)
                    stat, statb = sb("stat", [128, 8], F32, st)
                    wf = [sb("wf%d" % i, [128, 32, 128], BF16, st) for i in range(3)]
                    wt = [sb("wt%d" % i, [128, 32, TMW], BF16, st) for i in range(2)]
                    P1T = L.get("p1")
                    if P1T is None:
                        P1T = [(t_, 512, "full") for t_ in range(0, NT, 512)]
                    HALO_FM = ("a_k", "a_lr", "c_k")
                    HALO_TM = ("a_k", "a_v", "c_v")
                    ftl, ttl = [], []
                    tile_f, tile_t = [], []
                    for (t_, T_, md) in P1T:
                        fj = [j for j, blk_ in enumerate(FMB) if md == "full" or blk_[0] in HALO_FM]
                        tj = [j for j, blk_ in enumerate(TMB) if md == "full" or blk_[0] in HALO_TM]
                        tile_f.append((len(ftl), fj))
                        tile_t.append((len(ttl), tj))
                        ftl += fj
                        ttl += tj

                    def fload(i, t, b):
                        j = ftl[i]
                        wd = FMB[j][3]
                        k.dma("sp", t[:].rearrange("p k c -> p (k c)")[:, 0:32 * wd], FMW[l][j, :, 0:32 * wd], w=[b])

                    def tload(i, t, b):
                        k.dma("sp", t[:].rearrange("p k c -> p (k c)"), TMWs[l][ttl[i]], w=[b])

                    fring = Ring(wf, fload, len(ftl))
                    tring = Ring(wt, tload, len(ttl))
                    qtasks = []
                    if l + 1 < depth:
                        ln_ = l + 1
                        for j, (g, b, c0, wd) in enumerate(FMB):
                            for kq in range(4):
                                src = w_in[ln_, kq * 1024:(kq + 1) * 1024, c0:c0 + wd].rearrange("(k p) c -> p k c", p=128)
                                dstv = FMW[ln_][j, :, 0:32 * wd].rearrange("p (k c) -> p k c", c=wd)[:, kq * 8:(kq + 1) * 8, :]
                                qtasks.append((src, dstv, wd))
                        for f in range(32):
                            for kq in range(4):
                                src = w_br[ln_, kq * 1024:(kq + 1) * 1024, f * 128:(f + 1) * 128].rearrange("(k p) c -> p k c", p=128)
                                dstv = FMW[ln_][len(FMB) + f].rearrange("p (k c) -> p k c", c=128)[:, kq * 8:(kq + 1) * 8, :]
                                qtasks.append((src, dstv, 128))
                        for j, (g, b, c0, wd) in enumerate(TMB):
                            for hh in range(2):
                                for kq in range(4):
                                    src = w_in[ln_, kq * 1024:(kq + 1) * 1024, c0 + hh * 128:c0 + (hh + 1) * 128].rearrange("(k p) c -> p k c", p=128)
                                    dstv = TMWs[ln_][j].rearrange("p (k c) -> p k c", c=TMW)[:, kq * 8:(kq + 1) * 8, hh * 128:(hh + 1) * 128]
                                    qtasks.append((src, dstv, 128))
                        for f in range(16):
                            for hh in range(2):
                                for kq in range(4):
                                    cc0 = f * TMW + hh * 128
                                    src = w_out[ln_, kq * 1024:(kq + 1) * 1024, cc0:cc0 + 128].rearrange("(k p) c -> p k c", p=128)
                                    dstv = TMWs[ln_][len(TMB) + f].rearrange("p (k c) -> p k c", c=TMW)[:, kq * 8:(kq + 1) * 8, hh * 128:(hh + 1) * 128]
                                    qtasks.append((src, dstv, 128))
                    qst = [sb("q_s%d" % i, [128, 8, 128], F32, st) for i in range(2)]
                    qot = [sb("q_o%d" % i, [128, 8, 128], BF16, st) for i in range(2)]
                    qring = Ring(qst, lambda i, t, b: k.dma("sp", t[:, :, 0:qtasks[i][2]], qtasks[i][0], w=[b]), len(qtasks))
                    qn = [0]

                    def conv_step():
                        i = qn[0]
                        if i >= len(qtasks):
                            return False
                        qn[0] += 1
                        src, dstv, wd = qtasks[i]
                        s_, sbf = qring.get(i)
                        o_, obf = qot[i % 2]
                        k.I("pool", lambda: nc.gpsimd.tensor_copy(out=o_[:, :, 0:wd], in_=s_[:, :, 0:wd]), r=[sbf], w=[obf])
                        qring.done(i)
                        k.dma("sp", dstv, o_[:, :, 0:wd], r=[obf])
                        return True

                    deferred = []
                    blkno = [0]
                    ofm = [sb("ofm%d" % i, [128, 512], BF16, st) for i in range(6)]
                    otm = [sb("otm%d" % i, [128, 4, TMW], BF16, st) for i in range(2)]
                    lrs, lrsb = sb("lrs", [32, 512], F32, st)
                    G, Gb = sb("G", [128, 4, 1024], F32, st)
                    vvo, vvob = junk[:].rearrange("p (a b) -> p a b", b=1024), junkb
                    lnj, lnjb = sb("lnj", [128, 1024], BF16, st)
                    lng, lngb = sb("lng", [128, 1024], F32, st)
                    lnb, lnbb = sb("lnb", [128, 1024], F32, st)
                    k.dma("sp", lng[:], ln_g[l:l + 1, :].to_broadcast([128, 1024]), w=[lngb])
                    k.dma("sp", lnb[:], ln_b[l:l + 1, :].to_broadcast([128, 1024]), w=[lnbb])
                    rc, rcb = sb("rc", [32, 512], F32, st)
                    rs, rsb = sb("rs", [32, 512], F32, st)
                    sq, sqb = sb("sq", [128, 512], BF16, st)
                    rstd, rstdb = sb("rstd", [128, 512], F32, st)
                    t1, t1b = sb("t1", [32, 512], F32, st)
                    t2, t2b = sb("t2", [32, 512], F32, st)
                    lnst, lnstb = sb("lnst", [128, 8], F32, st)
                    cn = dict(wf=0, wt=0, ofm=0, otm=0)

                    for ti, (tok0, T, tmode) in enumerate(P1T):
                        NSUB = T // 128
                        fbase, fjl = tile_f[ti]
                        tbase, tjl = tile_t[ti]
                        k.dma("sp", rc[:, 0:T], ropec[l, :, tok0:tok0 + T], w=[rcb])
                        k.dma("sp", rs[:, 0:T], ropes[l, :, tok0:tok0 + T], w=[rsb])
                        for sub in range(NSUB):
                            (xl, xlb), (xn, xnbb) = xld[0], xnb[0]
                            r0 = tok0 + sub * 128
                            k.dma("sp", xl[:], xsrc[r0:r0 + 128, :], w=[xlb])
                            k.I("act", lambda: nc.scalar.activation(out=xn[:], in_=xl[:], func=AF.Square), r=[xlb], w=[xnbb])
                            k.I("dve", lambda: nc.vector.reduce_sum(out=stat[:, sub:sub + 1], in_=xn[:], axis=AX.X), r=[xnbb], w=[statb])
                            k.I("dve", lambda: nc.vector.tensor_scalar(out=stat[:, 4 + sub:5 + sub], in0=stat[:, sub:sub + 1],
                                                                       scalar1=1.0 / D, scalar2=1e-6, op0=ALU.mult, op1=ALU.add),
                                r=[statb], w=[statb])
                            k.I("act", lambda: nc.scalar.activation(out=stat[:, 4 + sub:5 + sub], in_=stat[:, 4 + sub:5 + sub], func=AF.Sqrt),
                                r=[statb], w=[statb])
                            k.I("dve", lambda: nc.vector.reciprocal(out=stat[:, 4 + sub:5 + sub], in_=stat[:, 4 + sub:5 + sub]),
                                r=[statb], w=[statb])
                            k.I("dve", lambda: nc.vector.scalar_tensor_tensor(out=xn[:], in0=xl[:], scalar=stat[:, 4 + sub:5 + sub],
                                                                              in1=gain_bc[:], op0=ALU.mult, op1=ALU.mult),
                                r=[xlb, statb, gain_bcb], w=[xnbb])
                            for kg in range(8):
                                pa, pb = ps(1)
                                for q4 in range(4):
                                    kc = kg * 4 + q4
                                    k.I("pe", lambda: nc.tensor.matmul(pa[:, q4 * 128:(q4 + 1) * 128], lhsT=xn[:, kc * 128:(kc + 1) * 128],
                                                                       rhs=ident_h, start=True, stop=True),
                                        r=[xnbb, cst_hb], w=pb)
                                dst = xnT[:, kg * 4:(kg + 1) * 4, sub * 128:(sub + 1) * 128]
                                src = pa.rearrange("p (a b) -> p a b", b=128)
                                if kg % 2 == 0:
                                    k.I("act", lambda: nc.scalar.copy(out=dst, in_=src), r=pb, w=[xnTb])
                                else:
                                    k.I("dve", lambda: nc.vector.tensor_copy(out=dst, in_=src), r=pb, w=[xnTb])

                        if chk("p1b"):

                            return True
                        for fi_, j in enumerate(fjl):
                            g, b, c0, wd = FMB[j]
                            if chk("p1c%d" % j):
                                return True
                            wtile, wb_ = fring.get(fbase + fi_)
                            wv = wtile[:].rearrange("p k c -> p (k c)")[:, 0:32 * wd].rearrange("p (k c) -> p k c", c=wd)
                            pa, pb = ps(1)
                            for kc in range(32):
                                k.I("pe", lambda: nc.tensor.matmul(pa[0:wd, 0:T], lhsT=wv[:, kc, :], rhs=xnT[:, kc, 0:T],
                                                                   start=(kc == 0), stop=(kc == 31)),
                                    r=[wb_, xnTb], w=pb)
                            fring.done(fbase + fi_)
                            blkno[0] += 1
                            while deferred and deferred[0][0] <= blkno[0]:
                                deferred.pop(0)[1]()
                            conv_step()
                            if g == "a_lr":
                                k.I("act", lambda: nc.scalar.copy(out=lrs[:, 0:T], in_=pa[0:32, 0:T]), r=pb, w=[lrsb])
                                k.dma("pool", LRT[:, tok0:tok0 + T], lrs[:, 0:T], r=[lrsb])
                                continue
                            o, ob = ofm[cn["ofm"] % 6]
                            cn["ofm"] += 1
                            if g == "a_q":
                                k.I("act", lambda: nc.scalar.activation(out=o[:, 0:T], in_=pa[:, 0:T], func=AF.Copy, scale=float(128.0 ** -0.5)),
                                    r=pb, w=[ob])
                            elif g == "a_k":
                                k.I("dve", lambda: nc.vector.tensor_copy(out=o[:, 0:T], in_=pa[:, 0:T]), r=pb, w=[ob])
                            elif g in ("a_gate", "b_gate", "c_gate"):
                                k.I("act", lambda: nc.scalar.activation(out=o[:, 0:T], in_=pa[:, 0:T], func=AF.Silu), r=pb, w=[ob])
                            elif g == "b_u":
                                k.I("act", lambda: nc.scalar.activation(out=o[:, 0:T], in_=pa[:, 0:T], func=AF.Gelu_apprx_tanh), r=pb, w=[ob])
                            elif g == "g":
                                gi = b
                                k.I("act", lambda: nc.scalar.activation(out=o[:, 0:T], in_=pa[:, 0:T], func=AF.Sigmoid, bias=gb_t[:, gi:gi + 1]),
                                    r=pb + [gb_tb], w=[ob])
                            else:
                                gcol = 0 if g == "c_q" else 1
                                k.I("act", lambda: nc.scalar.activation(out=sq[:, 0:T], in_=pa[:, 0:T], func=AF.Square), r=pb, w=[sqb])

                                def C1(pa=pa, pb=pb, o=o, ob=ob, gcol=gcol, row=FM_OFF[g] + b * 128, tok0=tok0, T=T):
                                    pq, pqb = ps(1)
                                    k.I("pe", lambda: nc.tensor.matmul(pq[:, 0:T], lhsT=ones_h[:], rhs=sq[:, 0:T], start=True, stop=True),
                                        r=[ones_hb, sqb], w=pqb)
                                    k.I("act", lambda: nc.scalar.activation(out=rstd[:, 0:T], in_=pq[:, 0:T], func=AF.Sqrt, bias=epsq[:, 0:1]), r=pqb + [epsqb], w=[rstdb])
                                    k.I("dve", lambda: nc.vector.reciprocal(out=rstd[:, 0:T], in_=rstd[:, 0:T]), r=[rstdb], w=[rstdb])
                                    k.I("dve", lambda: nc.vector.scalar_tensor_tensor(out=o[:, 0:T], in0=pa[:, 0:T], scalar=qkg[:, gcol:gcol + 1], in1=rstd[:, 0:T],
                                                                                      op0=ALU.mult, op1=ALU.mult),
                                        r=pb + [qkgb, rstdb], w=[ob])

                                    def C2():
                                        pr, prb = ps(1)
                                        k.I("pe", lambda: nc.tensor.matmul(pr[0:32, 0:T], lhsT=rot_h[:, 0:32], rhs=o[:, 0:T], start=True, stop=True),
                                            r=[cst_hb, ob], w=prb)
                                        k.I("dve", lambda: nc.vector.tensor_tensor(out=t1[:, 0:T], in0=pr[0:32, 0:T], in1=rs[:, 0:T], op=ALU.mult),
                                            r=prb + [rsb], w=[t1b])
                                        k.I("pool", lambda: nc.gpsimd.tensor_tensor(out=t2[:, 0:T], in0=o[0:32, 0:T], in1=rc[:, 0:T], op=ALU.mult),
                                            r=[ob, rcb], w=[t2b])
                                        k.I("dve", lambda: nc.vector.tensor_tensor(out=o[0:32, 0:T], in0=t1[:, 0:T], in1=t2[:, 0:T], op=ALU.add),
                                            r=[t1b, t2b], w=[ob])
                                        k.dma("sp", FM[row:row + 128, tok0:tok0 + T], o[:, 0:T], r=[ob])
                                    deferred.append((blkno[0] + 2, C2))
                                    deferred.sort(key=lambda x_: x_[0])
                                deferred.append((blkno[0] + 1, C1))
                                deferred.sort(key=lambda x_: x_[0])
                                continue
                            row = FM_OFF[g] + b * 128
                            k.dma("pool", FM[row:row + 128, tok0:tok0 + T], o[:, 0:T], r=[ob])

                        while deferred:
                            deferred.pop(0)[1]()
                        if chk("p1d"):
                            return True
                        for tq_, j in enumerate(tjl):
                            g, b, c0, wd = TMB[j]
                            if chk("p1e%d" % j):
                                return True
                            wtile, wb_ = tring.get(tbase + tq_)
                            o, ob = otm[cn["otm"] % 2]
                            cn["otm"] += 1
                            for s2 in range(NSUB // 2):
                                pa, pb = ps(1)
                                for s1 in range(2):
                                    sub = s2 * 2 + s1
                                    for kc in range(32):
                                        k.I("pe", lambda: nc.tensor.matmul(pa[:, s1 * TMW:(s1 + 1) * TMW], lhsT=xnT[:, kc, sub * 128:(sub + 1) * 128],
                                                                           rhs=wtile[:, kc, :], start=(kc == 0), stop=(kc == 31)),
                                            r=[wb_, xnTb], w=pb)
                                if s2 == NSUB // 2 - 1:
                                    tring.done(tbase + tq_)
                                    conv_step()
                                src = pa.rearrange("p (a b) -> p a b", b=TMW)
                                if g == "b_v":
                                    k.I("act", lambda: nc.scalar.activation(out=G[:, s2 * 2:s2 * 2 + 2, b * TMW:(b + 1) * TMW], in_=src,
                                                                            func=AF.Gelu_apprx_tanh), r=pb, w=[Gb])
                                else:
                                    k.I("dve", lambda: nc.vector.tensor_copy(out=o[:, s2 * 2:s2 * 2 + 2, :], in_=src), r=pb, w=[ob])
                            if g != "b_v":
                                col = TM_OFF[g] + b * TMW
                                k.dma("pool", TM[tok0:tok0 + T, col:col + TMW].rearrange("(s p) c -> p s c", p=128), o[:, 0:NSUB, :], r=[ob])
                            elif b == 3:
                                for sub in range(NSUB):
                                    k.I("dve", lambda: nc.vector.reduce_sum(out=lnst[:, 0:1], in_=G[:, sub, :], axis=AX.X), r=[Gb], w=[lnstb])
                                    k.I("act", lambda: nc.scalar.activation(out=lnj[:], in_=G[:, sub, :], func=AF.Square), r=[Gb], w=[lnjb])
                                    k.I("dve", lambda: nc.vector.reduce_sum(out=lnst[:, 1:2], in_=lnj[:], axis=AX.X), r=[lnjb], w=[lnstb])
                                    k.I("dve", lambda: nc.vector.tensor_scalar_mul(out=lnst[:, 2:3], in0=lnst[:, 0:1], scalar1=1.0 / 1024),
                                        r=[lnstb], w=[lnstb])
                                    k.I("dve", lambda: nc.vector.tensor_tensor(out=lnst[:, 3:4], in0=lnst[:, 2:3], in1=lnst[:, 2:3], op=ALU.mult),
                                        r=[lnstb], w=[lnstb])
                                    k.I("dve", lambda: nc.vector.scalar_tensor_tensor(out=lnst[:, 4:5], in0=lnst[:, 1:2], scalar=1.0 / 1024,
                                                                                      in1=lnst[:, 3:4], op0=ALU.mult, op1=ALU.subtract),
                                        r=[lnstb], w=[lnstb])
                                    k.I("act", lambda: nc.scalar.activation(out=lnst[:, 5:6], in_=lnst[:, 4:5], func=AF.Sqrt, bias=epsq[:, 1:2]),
                                        r=[lnstb, epsqb], w=[lnstb])
                                    k.I("dve", lambda: nc.vector.reciprocal(out=lnst[:, 5:6], in_=lnst[:, 5:6]), r=[lnstb], w=[lnstb])
                                    k.I("dve", lambda: nc.vector.tensor_scalar(out=G[:, sub, :], in0=G[:, sub, :], scalar1=lnst[:, 2:3],
                                                                               scalar2=lnst[:, 5:6], op0=ALU.subtract, op1=ALU.mult),
                                        r=[Gb, lnstb], w=[Gb])
                                    k.I("pool", lambda: nc.gpsimd.tensor_tensor(out=G[:, sub, :], in0=G[:, sub, :], in1=lng[:], op=ALU.mult),
                                        r=[Gb, lngb], w=[Gb])
                                    k.I("pool", lambda: nc.gpsimd.tensor_tensor(out=vvo[:, sub, :], in0=G[:, sub, :], in1=lnb[:], op=ALU.add),
                                        r=[Gb, lnbb], w=[vvob])
                                col = TM_OFF["b_v"]
                                k.dma("pool", TM[tok0:tok0 + T, col:col + 1024].rearrange("(s p) c -> p s c", p=128), vvo[:, 0:NSUB, :], r=[vvob])
                    while conv_step():
                        pass
                k.reset()

                if stop == "p1":
                    return True
                phase2(nc, k, sb, ps, l, L, dict(FM=FM, LRT=LRT, TM=TM, OFM=OFM, OFW=OFW, gate_up=gate_up, gate_b=gate_b,
                                                 gla_ng=gla_ng, sgu_w=sgu_w, sgu_b=sgu_b, sink=sink, kvalid=kvalid,
                                                 cst_f=cst_f, cst_fb=cst_fb, cst_h=cst_h, cst_hb=cst_hb,
                                                 ones_h=ones_h, ones_hb=ones_hb, epsq=epsq, epsqb=epsqb))
                k.reset()

                if stop == "p2":
                    return True
                with contextlib.ExitStack() as st:
                    Ots = [sb("Ot%d" % i, [128, 32, 512], BF16, st) for i in range(2)]
                    Mt, Mtb = sb("Mt", [128, 32, 512], BF16, st)
                    wf = [sb("p3wf%d" % i, [128, 32, 128], BF16, st) for i in range(4)]
                    wt = [sb("p3wt%d" % i, [128, 32, TMW], BF16, st) for i in range(2)]
                    gt = [sb("p3g%d" % i, [128, 3, 512], BF16, st) for i in range(3)]
                    m1 = [sb("p3m%d" % i, [128, 512], F32, st) for i in range(3)]
                    xr = [sb("p3x%d" % i, [128, 4, TMW], F32, st) for i in range(3)]
                    yo = [sb("p3y%d" % i, [128, 4, TMW], F32, st) for i in range(2)]
                    P3 = L["p3"]
                    np3 = len(P3)
                    g0_ = FM_OFF["g"]

                    def oload(i, t, b):
                        tk = P3[i][0]
                        k.dma("sp", t[:], OFM[:, tk:tk + 512].rearrange("(k p) t -> p k t", p=128), w=[b])

                    def wfload(i, t, b):
                        k.dma("sp", t[:].rearrange("p k c -> p (k c)"), FMW[l][len(FMB) + i % 32], w=[b])

                    def gload(i, t, b):
                        tk = P3[i // 32][0]
                        k.dma("sp", t[:], FM[g0_:g0_ + 3 * 4096, tk:tk + 512].rearrange("(i r p) t -> p i r t", i=3, p=128)[:, :, i % 32, :], w=[b])

                    def wtload(i, t, b):
                        k.dma("sp", t[:].rearrange("p k c -> p (k c)"), TMWs[l][len(TMB) + i % 16], w=[b])

                    def xload(i, t, b):
                        tk = P3[i // 16][0]
                        f2 = i % 16
                        k.dma("sp", t[:], xsrc[tk:tk + 512, f2 * TMW:(f2 + 1) * TMW].rearrange("(s p) c -> p s c", p=128), w=[b])

                    oring = Ring(Ots, oload, np3)
                    wfring = Ring(wf, wfload, np3 * 32)
                    gring = Ring(gt, gload, np3 * 32)
                    wtring = Ring(wt, wtload, np3 * 16)
                    xring = Ring(xr, xload, np3 * 16)
                    ycnt = 0
                    for pi, (tok0, dname, drow) in enumerate(P3):
                        dst = outmap[dname]
                        Ot, Otb = oring.get(pi)
                        for f in range(32):
                            wtile, wb_ = wfring.get(pi * 32 + f)
                            g3, g3b = gring.get(pi * 32 + f)
                            pss = []
                            for bi, (k0, k1) in enumerate(((0, 12), (12, 20), (20, 32))):
                                pa, pb = ps(1)
                                pss.append((pa, pb))
                                for kc in range(k0, k1):
                                    k.I("pe", lambda: nc.tensor.matmul(pa, lhsT=wtile[:, kc, :], rhs=Ot[:, kc, :],
                                                                       start=(kc == k0), stop=(kc == k1 - 1)),
                                        r=[wb_, Otb], w=pb)
                            wfring.done(pi * 32 + f)
                            if f == 31:
                                oring.done(pi)
                            for bi in range(3):
                                pa, pb = pss[bi]
                                mm, mmb = m1[bi]
                                k.I("dve", lambda: nc.vector.tensor_tensor(out=mm[:], in0=pa, in1=g3[:, bi, :], op=ALU.mult),
                                    r=pb + [g3b], w=[mmb])
                            gring.done(pi * 32 + f)
                            k.I("pool", lambda: nc.gpsimd.tensor_tensor(out=m1[0][0][:], in0=m1[0][0][:], in1=m1[1][0][:], op=ALU.add),
                                r=[m1[0][1], m1[1][1]], w=[m1[0][1]])
                            k.I("pool", lambda: nc.gpsimd.tensor_tensor(out=Mt[:, f, :], in0=m1[0][0][:], in1=m1[2][0][:], op=ALU.add),
                                r=[m1[0][1], m1[2][1]], w=[Mtb])
                        for f2 in range(16):
                            wtile, wb_ = wtring.get(pi * 16 + f2)
                            xt_, xtb = xring.get(pi * 16 + f2)
                            yt_, ytb = yo[ycnt % 2]
                            ycnt += 1
                            for s2 in range(2):
                                pa, pb = ps(1)
                                for s1 in range(2):
                                    sub = s2 * 2 + s1
                                    for kc in range(32):
                                        k.I("pe", lambda: nc.tensor.matmul(pa[:, s1 * TMW:(s1 + 1) * TMW], lhsT=Mt[:, kc, sub * 128:(sub + 1) * 128],
                                                                           rhs=wtile[:, kc, :], start=(kc == 0), stop=(kc == 31)),
                                            r=[wb_, Mtb], w=pb)
                                if s2 == 1:
                                    wtring.done(pi * 16 + f2)
                                k.I("dve", lambda: nc.vector.tensor_tensor(out=yt_[:, s2 * 2:s2 * 2 + 2, :], in0=pa.rearrange("p (a b) -> p a b", b=TMW),
                                                                           in1=xt_[:, s2 * 2:s2 * 2 + 2, :], op=ALU.add),
                                    r=pb + [xtb], w=[ytb])
                            xring.done(pi * 16 + f2)
                            k.dma("sp", dst[drow:drow + 512, f2 * TMW:(f2 + 1) * TMW].rearrange("(s p) c -> p s c", p=128), yt_[:], r=[ytb])
                k.reset()
            return False

        for l in range(depth if stop != "pre" else 0):
            if do_layer(l):
                break
        k.barrier()
        print("instructions:", k.ninst, "sems:", k.nsem, {e: k.seq[e] for e in k.seq})
    return nc


def phase2(nc, k, sb, ps, l, L, T):
    FM, LRT, TM, OFM, OFW = T["FM"], T["LRT"], T["TM"], T["OFM"], T["OFW"]
    cst_f, cst_fb, cst_h, cst_hb = T["cst_f"], T["cst_fb"], T["cst_h"], T["cst_hb"]
    ones_h, ones_hb = T["ones_h"], T["ones_hb"]
    epsq, epsqb = T["epsq"], T["epsqb"]
    ident_f = cst_f[:, 0:128]
    masks = {0: cst_f[:, 128:256], 1: cst_f[:, 256:384]}
    masks_h = {0: cst_h[:, 128:256], 1: cst_h[:, 256:384]}
    cums = {0: cst_f[:, 384:512], 1: cst_f[:, 512:640]}

    with contextlib.ExitStack() as st:
        Gaug = []
        for dr in range(2):
            t, tb = sb("Gaug%d" % dr, [33, 768], F32, st)
            k.I("dve", lambda: nc.vector.memset(t[:], 0.0), w=[tb])
            k.dma("sp", t[dr * 16:(dr + 1) * 16, :], T["gate_up"][l, dr], w=[tb])
            k.dma("sp", t[32:33, :], T["gate_b"][l, dr:dr + 1, :], w=[tb])
            Gaug.append((t, tb))
        gng, gngb = sb("gng", [128, 2], F32, st)
        k.dma("sp", gng[:], T["gla_ng"][l].rearrange("(j p) -> p j", p=128), w=[gngb], allow_slow_non_contiguous=True)
        qTs = [sb("g_qT%d" % i, [128, 6, 512], BF16, st) for i in range(2)]
        kTs = [sb("g_kT%d" % i, [128, 6, 512], BF16, st) for i in range(2)]
        lrs_ = [sb("g_lr%d" % i, [33, 512], F32, st) for i in range(2)]
        kvs = [sb("g_kv%d" % i, [128, 4, 2304], BF16, st) for i in range(2)]
        ag, agb = sb("g_ag", [128, 12, 512], BF16, st)
        of_, ofb = sb("g_of", [128, 12, 512], F32, st)
        ob_, obb = sb("g_ob", [128, 12, 512], F32, st)
        S, Sb = sb("g_S", [128, 6, 256], F32, st)
        Sh, Shb = sb("g_Sh", [128, 6, 256], BF16, st)
        tmpS, tmpSb = sb("g_tmpS", [128, 6, 256], F32, st)
        ez, ezb = sb("g_ez", [128, 768], F32, st)
        lz, lzb = sb("g_lz", [128, 768], F32, st)
        PR = []
        for i in range(2):
            d = {}
            for nm, dt_ in (("Ef", F32), ("Em", F32), ("Emt", F32), ("qs", BF16), ("ks", BF16), ("kst", BF16)):
                d[nm] = sb("g_%s%d" % (nm, i), [128, 768], dt_, st)
            PR.append(d)
        At, Atb = sb("g_At", [128, 768], BF16, st)
        sqh, sqhb = sb("g_sq", [128, 12, 512], BF16, st)
        rr, rrb = sb("g_rr", [128, 512], F32, st)
        for (t_, tb_) in lrs_:
            k.I("dve", lambda: nc.vector.memset(t_[:], 1.0), w=[tb_])

        for (s0, slen) in L["segs"]:
            nsc = slen // 512
            for dr in range(2):
                k.I("dve", lambda: nc.vector.memset(S[:], 0.0), w=[Sb])
                k.I("pool", lambda: nc.gpsimd.memset(Sh[:], 0.0), w=[Shb])
                last = 127 if dr == 0 else 0
                scs = list(range(nsc)) if dr == 0 else list(range(nsc - 1, -1, -1))
                chs = list(range(4)) if dr == 0 else list(range(3, -1, -1))

                def T0(i):
                    return s0 + scs[i] * 512

                rq = Ring(qTs, lambda i, t, b: k.dma("sp", t[:], FM[FM_OFF["a_q"]:FM_OFF["a_q"] + 768, T0(i):T0(i) + 512].rearrange("(h p) t -> p h t", p=128), w=[b]), nsc)
                rk = Ring(kTs, lambda i, t, b: k.dma("sp", t[:], FM[FM_OFF["a_k"]:FM_OFF["a_k"] + 768, T0(i):T0(i) + 512].rearrange("(h p) t -> p h t", p=128), w=[b]), nsc)
                rl = Ring(lrs_, lambda i, t, b: k.dma("sp", t[0:32, :], LRT[:, T0(i):T0(i) + 512], w=[b]), nsc)
                rv = Ring(kvs, lambda i, t, b: k.dma("sp", t[:], TM[T0(i):T0(i) + 512, 0:2304].rearrange("(c p) f -> p c f", p=128), w=[b]), nsc)
                items = [(i, ch) for i in range(nsc) for ch in chs]

                def prep(n):
                    i, ch = items[n]
                    P = PR[n % 2]
                    qT, qTb = rq.get(i)
                    kT, kTb = rk.get(i)
                    lr, lrb = rl.get(i)
                    kv, kvb = rv.get(i)
                    tsl = slice(ch * 128, (ch + 1) * 128)
                    pz, pzb = ps(2)
                    for hf in range(2):
                        k.I("pe", lambda: nc.tensor.matmul(pz[:, hf * 512:hf * 512 + 384], lhsT=lr[0:33, tsl], rhs=Gaug[dr][0][:, hf * 384:(hf + 1) * 384],
                                                           start=True, stop=True), r=[lrb, Gaug[dr][1]], w=pzb)
                    for hf in range(2):
                        k.I("act", lambda: nc.scalar.activation(out=ez[:, hf * 384:(hf + 1) * 384], in_=pz[:, hf * 512:hf * 512 + 384],
                                                                func=AF.Exp, scale=-1.0), r=pzb, w=[ezb])
                    k.I("act", lambda: nc.scalar.activation(out=lz[:], in_=ez[:], func=AF.Ln, bias=epsq[:, 3:4]), r=[ezb, epsqb], w=[lzb])
                    pbT, pbTb = ps(2)
                    for h in range(6):
                        off = (h // 4) * 512 + (h % 4) * 128
                        k.I("pe", lambda: nc.tensor.matmul(pbT[:, off:off + 128], lhsT=lz[:, h * 128:(h + 1) * 128], rhs=cums[dr],
                                                           start=True, stop=True), r=[lzb, cst_fb], w=pbTb)
                    pbt, pbtb = ps(2)
                    for hf in range(2):
                        k.I("pe", lambda: nc.tensor.matmul(pbt[:, hf * 512:hf * 512 + 384], lhsT=cums[dr], rhs=lz[:, hf * 384:(hf + 1) * 384],
                                                           start=True, stop=True), r=[lzb, cst_fb], w=pbtb)
                    for (nm, sc_) in (("Ef", 1.0), ("Em", -1.0)):
                        o_, ob2 = P[nm]
                        k.I("act", lambda: nc.scalar.activation(out=o_[:, 0:512], in_=pbT[:, 0:512], func=AF.Exp, scale=sc_), r=pbTb, w=[ob2])
                        k.I("act", lambda: nc.scalar.activation(out=o_[:, 512:768], in_=pbT[:, 512:768], func=AF.Exp, scale=sc_), r=pbTb, w=[ob2])
                    Emt, Emtb = P["Emt"]
                    for hf in range(2):
                        k.I("act", lambda: nc.scalar.activation(out=Emt[:, hf * 384:(hf + 1) * 384], in_=pbt[:, hf * 512:hf * 512 + 384],
                                                                func=AF.Exp, scale=-1.0), r=pbtb, w=[Emtb])
                    (Ef, Efb), (Em, Emb) = P["Ef"], P["Em"]
                    (qs, qsb), (ks, ksb), (kst, kstb) = P["qs"], P["ks"], P["kst"]
                    k.I("dve", lambda: nc.vector.tensor_tensor(out=qs[:].rearrange("p (h t) -> p h t", t=128), in0=qT[:, :, tsl],
                                                               in1=Ef[:].rearrange("p (h t) -> p h t", t=128), op=ALU.mult),
                        r=[qTb, Efb], w=[qsb])
                    k.I("dve", lambda: nc.vector.tensor_tensor(out=ks[:].rearrange("p (h t) -> p h t", t=128), in0=kT[:, :, tsl],
                                                               in1=Em[:].rearrange("p (h t) -> p h t", t=128), op=ALU.mult),
                        r=[kTb, Emb], w=[ksb])
                    k.I("pool", lambda: nc.gpsimd.tensor_tensor(out=kst[:], in0=kv[:, ch, 0:768], in1=Emt[:], op=ALU.mult),
                        r=[kvb, Emtb], w=[kstb])
                    if ch == chs[-1]:
                        rq.done(i)
                        rk.done(i)
                        rl.done(i)

                def main(n):
                    i, ch = items[n]
                    P = PR[n % 2]
                    kv, kvb = rv.get(i)
                    t0 = T0(i)
                    tsl = slice(ch * 128, (ch + 1) * 128)
                    (Ef, Efb) = P["Ef"]
                    (qs, qsb), (ks, ksb), (kst, kstb) = P["qs"], P["ks"], P["kst"]
                    if ch == chs[0] and dr == 1:
                        k.dma("sp", ag[:], FM[FM_OFF["a_gate"]:FM_OFF["a_gate"] + 1536, t0:t0 + 512].rearrange("(j p) t -> p j t", p=128), w=[agb])
                        k.dma("sp", of_[:], OFW[:, t0:t0 + 512].rearrange("(j p) t -> p j t", p=128), w=[ofb])
                    pA, pAb = ps(2)
                    for h in range(6):
                        off = (h // 4) * 512 + (h % 4) * 128
                        k.I("pe", lambda: nc.tensor.matmul(pA[:, off:off + 128], lhsT=ks[:, h * 128:(h + 1) * 128], rhs=qs[:, h * 128:(h + 1) * 128],
                                                           start=True, stop=True), r=[ksb, qsb], w=pAb)
                    k.I("dve", lambda: nc.vector.tensor_tensor(out=At[:, 0:512].rearrange("p (h t) -> p h t", t=128),
                                                               in0=pA[:, 0:512].rearrange("p (h t) -> p h t", t=128),
                                                               in1=masks[dr].unsqueeze(1).to_broadcast([128, 4, 128]), op=ALU.mult),
                        r=pAb + [cst_fb], w=[Atb])
                    k.I("dve", lambda: nc.vector.tensor_tensor(out=At[:, 512:768].rearrange("p (h t) -> p h t", t=128),
                                                               in0=pA[:, 512:768].rearrange("p (h t) -> p h t", t=128),
                                                               in1=masks[dr].unsqueeze(1).to_broadcast([128, 2, 128]), op=ALU.mult),
                        r=pAb + [cst_fb], w=[Atb])
                    pks = []
                    for h3 in range(3):
                        pk, pkb = ps(1)
                        pks.append((pk, pkb))
                        for hh in range(2):
                            h = h3 * 2 + hh
                            k.I("pe", lambda: nc.tensor.matmul(pk[:, hh * 256:(hh + 1) * 256], lhsT=kst[:, h * 128:(h + 1) * 128],
                                                               rhs=kv[:, ch, 768 + h * 256:768 + (h + 1) * 256], start=True, stop=True),
                                r=[kstb, kvb], w=pkb)
                        k.I("dve", lambda: nc.vector.tensor_tensor(out=tmpS[:, h3 * 2:h3 * 2 + 2, :], in0=pk.rearrange("p (a b) -> p a b", b=256),
                                                                   in1=S[:, h3 * 2:h3 * 2 + 2, :], op=ALU.add), r=pkb + [Sb], w=[tmpSb])
                    dsto = of_ if dr == 0 else ob_
                    dstob = ofb if dr == 0 else obb
                    for h3 in range(3):
                        po, pob = ps(1)
                        for hh in range(2):
                            h = h3 * 2 + hh
                            for j in range(2):
                                c = (hh * 2 + j) * 128
                                k.I("pe", lambda: nc.tensor.matmul(po[:, c:c + 128], lhsT=Sh[:, h, j * 128:(j + 1) * 128], rhs=qs[:, h * 128:(h + 1) * 128],
                                                                   start=True, stop=False), r=[Shb, qsb], w=pob)
                                k.I("pe", lambda: nc.tensor.matmul(po[:, c:c + 128], lhsT=kv[:, ch, 768 + h * 256 + j * 128:768 + h * 256 + (j + 1) * 128],
                                                                   rhs=At[:, h * 128:(h + 1) * 128], start=False, stop=True), r=[kvb, Atb], w=pob)
                        k.I("act", lambda: nc.scalar.copy(out=dsto[:, h3 * 4:(h3 + 1) * 4, tsl], in_=po.rearrange("p (a b) -> p a b", b=128)),
                            r=pob, w=[dstob])
                    ebl = Ef[:].rearrange("p (h t) -> p h t", t=128)[:, :, last:last + 1].to_broadcast([128, 6, 256])
                    k.I("dve", lambda: nc.vector.tensor_tensor(out=Sh[:], in0=tmpS[:], in1=ebl, op=ALU.mult), r=[tmpSb, Efb], w=[Shb])
                    k.I("dve", lambda: nc.vector.tensor_tensor(out=S[:], in0=tmpS[:], in1=ebl, op=ALU.mult), r=[tmpSb, Efb], w=[Sb])
                    if ch != chs[-1]:
                        return
                    rv.done(i)
                    if dr == 0:
                        k.dma("sp", OFW[:, t0:t0 + 512].rearrange("(j p) t -> p j t", p=128), of_[:], r=[ofb])
                    else:
                        k.I("pool", lambda: nc.gpsimd.tensor_tensor(out=ob_[:], in0=ob_[:], in1=of_[:], op=ALU.add), r=[obb, ofb], w=[obb])
                        k.I("act", lambda: nc.scalar.activation(out=sqh[:], in_=ob_[:], func=AF.Square), r=[obb], w=[sqhb])
                        for h in range(6):
                            pn, pnb = ps(1)
                            for j in range(2):
                                k.I("pe", lambda: nc.tensor.matmul(pn, lhsT=ones_h[:], rhs=sqh[:, h * 2 + j, :], start=(j == 0), stop=(j == 1)),
                                    r=[ones_hb, sqhb], w=pnb)
                            k.I("act", lambda: nc.scalar.activation(out=rr[:], in_=pn, func=AF.Ln, bias=epsq[:, 2:3], scale=1.0 / 256), r=pnb + [epsqb], w=[rrb])
                            k.I("act", lambda: nc.scalar.activation(out=rr[:], in_=rr[:], func=AF.Exp, scale=-0.5), r=[rrb], w=[rrb])
                            for j in range(2):
                                k.I("dve", lambda: nc.vector.scalar_tensor_tensor(out=ob_[:, h * 2 + j, :], in0=ob_[:, h * 2 + j, :], scalar=gng[:, j:j + 1],
                                                                                  in1=rr[:], op0=ALU.mult, op1=ALU.mult), r=[obb, gngb, rrb], w=[obb])
                        k.I("pool", lambda: nc.gpsimd.tensor_tensor(out=sqh[:], in0=ob_[:], in1=ag[:], op=ALU.mult), r=[obb, agb], w=[sqhb])
                        k.dma("sp", OFM[0:1536, t0:t0 + 512].rearrange("(j p) t -> p j t", p=128), sqh[:], r=[sqhb])

                nit = len(items)
                prep(0)
                for n in range(nit):
                    if n + 1 < nit:
                        prep(n + 1)
                    main(n)
                if dr == 0:
                    k.reset()

    with contextlib.ExitStack() as st:
        wraw, wrawb = sb("s_wraw", [128, 8, 128], F32, st)
        WT, WTb = sb("s_WT", [128, 8, 128], BF16, st)
        Bg, Bgb = sb("s_Bg", [128, 8, 128], F32, st)
        k.dma("sp", wraw[:], T["sgu_w"][l].rearrange("g p q -> p g q"), w=[wrawb])
        k.dma("sp", Bg[:].rearrange("p g q -> p (g q)"), T["sgu_b"][l:l + 1].rearrange("o g q -> o (g q)").to_broadcast([128, 1024]), w=[Bgb])
        for g in range(8):
            pa, pb = ps(1)
            k.I("pe", lambda: nc.tensor.matmul(pa[:, 0:128], lhsT=wraw[:, g, :], rhs=ident_f, start=True, stop=True), r=[wrawb, cst_fb], w=pb)
            k.I("act", lambda: nc.scalar.copy(out=WT[:, g, :], in_=pa[:, 0:128]), r=pb, w=[WTb])
        vv, vvb = sb("s_vv", [128, 4, 1024], BF16, st)
        uT, uTb = sb("s_u", [128, 8, 512], BF16, st)
        gT, gTb = sb("s_g", [128, 8, 512], BF16, st)
        tt, ttb = sb("s_t", [128, 512], F32, st)
        ug, ugb = sb("s_ug", [128, 8, 512], F32, st)
        obt, obtb = sb("s_ob", [128, 8, 512], BF16, st)
        sgu_tiles = [t_ for (t_, T_, md_) in L.get("p1", [(x_, 512, "full") for x_ in range(0, L["NT"], 512)]) if md_ == "full"]

        def sgu_gen():
            for t0 in sgu_tiles:
                k.dma("sp", vv[:], TM[t0:t0 + 512, TM_OFF["b_v"]:TM_OFF["b_v"] + 1024].rearrange("(c p) f -> p c f", p=128), w=[vvb])
                k.dma("sp", uT[:], FM[FM_OFF["b_u"]:FM_OFF["b_u"] + 1024, t0:t0 + 512].rearrange("(g p) t -> p g t", p=128), w=[uTb])
                k.dma("sp", gT[:], FM[FM_OFF["b_gate"]:FM_OFF["b_gate"] + 1024, t0:t0 + 512].rearrange("(g p) t -> p g t", p=128), w=[gTb])
                k.I("pool", lambda: nc.gpsimd.tensor_tensor(out=ug[:], in0=uT[:], in1=gT[:], op=ALU.mult), r=[uTb, gTb], w=[ugb])
                yield
                for g in range(8):
                    pa, pb = ps(1)
                    for c in range(4):
                        k.I("pe", lambda: nc.tensor.matmul(pa[:, c * 128:(c + 1) * 128], lhsT=vv[:, c, g * 128:(g + 1) * 128], rhs=WT[:, g, :],
                                                           start=True, stop=True), r=[vvb, WTb], w=pb)
                    k.I("dve", lambda: nc.vector.tensor_tensor(out=tt[:].rearrange("p (c q) -> p c q", q=128), in0=pa.rearrange("p (c q) -> p c q", q=128),
                                                               in1=Bg[:, g:g + 1, :].to_broadcast([128, 4, 128]), op=ALU.add), r=pb + [Bgb], w=[ttb])
                    k.I("dve", lambda: nc.vector.tensor_tensor(out=obt[:, g, :], in0=tt[:], in1=ug[:, g, :], op=ALU.mult), r=[ttb, ugb], w=[obtb])
                    yield
                k.dma("sp", OFM[1536:2560, t0:t0 + 512].rearrange("(g p) t -> p g t", p=128), obt[:], r=[obtb])
                yield

        sg_ = sgu_gen()

        def sgu_step():
            try:
                next(sg_)
                return True
            except StopIteration:
                return False

        NB = L["NT"] // 128
        kvm, kvmb = sb("a_kvm", [128, NB], F32, st)
        k.dma("sp", kvm[:], T["kvalid"][l, :, 0:NB], w=[kvmb])
        sk, skb = sb("a_sk", [128, 12], F32, st)
        k.dma("sp", sk[:], T["sink"][l:l + 1, :].to_broadcast([128, 12]), w=[skb])
        k.I("act", lambda: nc.scalar.activation(out=sk[:], in_=sk[:], func=AF.Exp), r=[skb], w=[skb])
        qTs = [sb("a_qT%d" % i, [128, 12, 512], BF16, st) for i in range(2)]
        kTs = [sb("a_kT%d" % i, [128, 4, 768], BF16, st) for i in range(2)]
        vts = [sb("a_v%d" % i, [128, 6, 512], BF16, st) for i in range(2)]
        cgs = [sb("a_cg%d" % i, [128, 12, 512], BF16, st) for i in range(2)]
        ocs = [sb("a_oc%d" % i, [128, 12, 512], BF16, st) for i in range(2)]
        Eb = [sb("a_E%d" % i, [128, 3, 384], BF16, st) for i in range(3)]
        rden, rdenb = sb("a_rd", [128, 384], F32, st)
        o1, o1b = sb("a_o1", [128, 384], F32, st)
        scale = float(128.0 ** -0.5)
        tiles = []
        for (s0, slen) in L["segs"]:
            for sc in range(slen // 512):
                tiles.append((s0 + sc * 512, s0, slen))

        def krange(i):
            t0, s0, slen = tiles[i]
            lo = max(t0 - 128, s0)
            hi = min(t0 + 640, s0 + slen)
            return lo, hi, lo - (t0 - 128)

        def kload(i, t, b):
            lo, hi, ko = krange(i)
            k.dma("sp", t[:, :, ko:ko + hi - lo], FM[FM_OFF["c_k"]:FM_OFF["c_k"] + 512, lo:hi].rearrange("(h p) t -> p h t", p=128), w=[b])

        def vload(i, t, b):
            lo, hi, ko = krange(i)
            k.dma("sp", t[:, ko // 128:(ko + hi - lo) // 128, :], TM[lo:hi, TM_OFF["c_v"]:TM_OFF["c_v"] + 512].rearrange("(c p) f -> p c f", p=128), w=[b])

        rq = Ring(qTs, lambda i, t, b: k.dma("sp", t[:], FM[FM_OFF["c_q"]:FM_OFF["c_q"] + 1536, tiles[i][0]:tiles[i][0] + 512].rearrange("(h p) t -> p h t", p=128), w=[b]), len(tiles))
        rk = Ring(kTs, kload, len(tiles))
        rv = Ring(vts, vload, len(tiles))
        rg = Ring(cgs, lambda i, t, b: k.dma("sp", t[:], FM[FM_OFF["c_gate"]:FM_OFF["c_gate"] + 1536, tiles[i][0]:tiles[i][0] + 512].rearrange("(h p) t -> p h t", p=128), w=[b]), len(tiles))
        items = [(i, blk, hk) for i in range(len(tiles)) for blk in range(4) for hk in range(4)]

        def jlist(i, blk):
            t0, s0, slen = tiles[i]
            q0 = t0 + blk * 128
            return [j for j in (-1, 0, 1) if s0 <= q0 + j * 128 < s0 + slen]

        def stageA(n):
            i, blk, hk = items[n]
            t0 = tiles[i][0]
            q0 = t0 + blk * 128
            qT, qTb = rq.get(i)
            kT, kTb = rk.get(i)
            Et, Etb = Eb[n % 3]
            qv = qT[:, hk * 3:(hk + 1) * 3, blk * 128:(blk + 1) * 128]
            for j in jlist(i, blk):
                kb = blk + 1 + j
                pa, pb = ps(1)
                k.I("pe", lambda: nc.tensor.matmul(pa[:, 0:384].rearrange("p (g t) -> p g t", t=128), lhsT=kT[:, hk, kb * 128:(kb + 1) * 128], rhs=qv,
                                                   start=True, stop=True), r=[kTb, qTb], w=pb)
                k.I("act", lambda: nc.scalar.activation(out=Et[:, j + 1, :], in_=pa[:, 0:384], func=AF.Exp, scale=scale), r=pb, w=[Etb])
                if j != 0:
                    mk = masks_h[1] if j == -1 else masks_h[0]
                    gblk = (q0 + j * 128) // 128
                    k.I("dve", lambda: nc.vector.scalar_tensor_tensor(out=Et[:, j + 1, :].rearrange("p (g t) -> p g t", t=128),
                                                                      in0=Et[:, j + 1, :].rearrange("p (g t) -> p g t", t=128),
                                                                      scalar=kvm[:, gblk:gblk + 1],
                                                                      in1=mk.unsqueeze(1).to_broadcast([128, 3, 128]),
                                                                      op0=ALU.mult, op1=ALU.mult), r=[Etb, kvmb, cst_hb], w=[Etb])
            if blk == 3 and hk == 3:
                rq.done(i)
                rk.done(i)

        def stageB(n):
            i, blk, hk = items[n]
            t0 = tiles[i][0]
            vt, vtb = rv.get(i)
            cg, cgb = rg.get(i)
            oc, ocb = ocs[i % 2]
            Et, Etb = Eb[n % 3]
            js = jlist(i, blk)
            po, pob = ps(1)
            pd, pdb = ps(1)
            for ji, j in enumerate(js):
                kb = blk + 1 + j
                k.I("pe", lambda: nc.tensor.matmul(po[:, 0:384], lhsT=vt[:, kb, hk * 128:(hk + 1) * 128], rhs=Et[:, j + 1, :],
                                                   start=(ji == 0), stop=(ji == len(js) - 1)), r=[vtb, Etb], w=pob)
            for ji, j in enumerate(js):
                k.I("pe", lambda: nc.tensor.matmul(pd[:, 0:384], lhsT=ones_h[:], rhs=Et[:, j + 1, :],
                                                   start=(ji == 0), stop=(ji == len(js) - 1)), r=[ones_hb, Etb], w=pdb)
            k.I("dve", lambda: nc.vector.tensor_tensor(out=rden[:].rearrange("p (g t) -> p g t", t=128), in0=pd[:, 0:384].rearrange("p (g t) -> p g t", t=128),
                                                       in1=sk[:, hk * 3:(hk + 1) * 3].unsqueeze(2).to_broadcast([128, 3, 128]), op=ALU.add),
                r=pdb + [skb], w=[rdenb])
            k.I("act", lambda: nc.scalar.activation(out=rden[:], in_=rden[:], func=AF.Ln), r=[rdenb], w=[rdenb])
            k.I("act", lambda: nc.scalar.activation(out=rden[:], in_=rden[:], func=AF.Exp, scale=-1.0), r=[rdenb], w=[rdenb])
            k.I("dve", lambda: nc.vector.tensor_tensor(out=o1[:], in0=po[:, 0:384], in1=rden[:], op=ALU.mult), r=pob + [rdenb], w=[o1b])
            k.I("pool", lambda: nc.gpsimd.tensor_tensor(out=oc[:, hk * 3:(hk + 1) * 3, blk * 128:(blk + 1) * 128],
                                                        in0=o1[:].rearrange("p (g t) -> p g t", t=128),
                                                        in1=cg[:, hk * 3:(hk + 1) * 3, blk * 128:(blk + 1) * 128], op=ALU.mult),
                r=[o1b, cgb], w=[ocb])
            if blk == 3 and hk == 3:
                rv.done(i)
                rg.done(i)
                k.dma("sp", OFM[2560:4096, t0:t0 + 512].rearrange("(h p) t -> p h t", p=128), oc[:], r=[ocb])

        nit = len(items)
        stageA(0)
        for n in range(nit):
            if n + 1 < nit:
                stageA(n + 1)
            stageB(n)
            if n % 2 == 1:
                sgu_step()
        while sgu_step():
            pass

def make_consts():
    s = np.arange(128)[:, None]
    t = np.arange(128)[None, :]
    ident = np.eye(128, dtype=np.float32)
    U = (s <= t).astype(np.float32)
    Lm = (s >= t).astype(np.float32)
    rot = np.zeros((128, 128), np.float32)
    for d in range(16):
        rot[16 + d, d] = 1.0
        rot[d, 16 + d] = 1.0
    return np.concatenate([ident, U, Lm, -U / 16.0, -Lm / 16.0, rot], axis=1).astype(np.float32)


def rope_tables(pos):
    inv = ROPE_THETA ** (-np.arange(0, 32, 2, dtype=np.float32) / 32.0)
    ang = pos.astype(np.float32)[None, :] * inv[:, None].astype(np.float32)
    c = np.cos(ang).astype(np.float32)
    s_ = np.sin(ang).astype(np.float32)
    return np.concatenate([c, c], 0), np.concatenate([-s_, s_], 0)


def full_cfg():
    H = HALO
    n0 = (2048 + 2 * H) // 512
    l0 = dict(NT=2048 + 2048 + 4 * H, segs=[(0, 2048), (2048, 2048 + 4 * H)], xsrc="x0",
              p1=[(i * 512, 512, "full") for i in range(4)] + [(2048, H, "halo")] +
                 [(2048 + H + i * 512, 512, "full") for i in range(n0)] + [(2048 + H + n0 * 512, H, "halo")],
              p3=[(i * 512, "x1", i * 512) for i in range(4)] +
                 [(2048 + H + i * 512, "x1", 2048 + i * 512) for i in range(n0)])
    l1 = dict(NT=2048 + 2048 + 2 * H, segs=[(0, 2048), (2048, 2048 + 2 * H)], xsrc="x1",
              p1=[(i * 512, 512, "full") for i in range(4)] + [(2048, H, "halo")] +
                 [(2048 + H + i * 512, 512, "full") for i in range(4)] + [(2048 + H + 2048, H, "halo")],
              p3=[(i * 512, "yp", i * 512) for i in range(4)] +
                 [(2048 + H + i * 512, "ys", i * 512) for i in range(4)])
    return dict(depth=2, layers=[l0, l1], outs=[("yp", 2048), ("ys", 2048)])


_CACHE = {}


def kernel(x_prompt, x_sample, norm_gain, w_in, gla_gate_up, gla_gate_bias, gla_norm_gain,
           sgu_ln_gain, sgu_ln_bias, sgu_w, sgu_b, q_norm_gain, k_norm_gain, sink,
           gate_bias, w_br, w_out):
    H = HALO
    cfg = full_cfg()
    f32 = lambda a: np.ascontiguousarray(np.asarray(a, dtype=np.float32))
    x_prompt, x_sample = f32(x_prompt), f32(x_sample)
    shared = dict(norm_gain=f32(norm_gain), w_in=f32(w_in), gla_gate_up=f32(gla_gate_up), gla_gate_bias=f32(gla_gate_bias),
                  gla_norm_gain=f32(gla_norm_gain), sgu_ln_gain=f32(sgu_ln_gain), sgu_ln_bias=f32(sgu_ln_bias),
                  sgu_w=f32(sgu_w), sgu_b=f32(sgu_b), q_norm_gain=f32(q_norm_gain), k_norm_gain=f32(k_norm_gain),
                  sink=f32(sink), gate_bias=f32(gate_bias), w_br=f32(w_br), w_out=f32(w_out), cst=make_consts())
    NT0 = cfg["layers"][0]["NT"]
    NBLK = NT0 // 128
    S = x_sample.shape[1]
    in_maps = []
    for c in range(8):
        x0 = np.zeros((NT0, D), np.float32)
        x0[0:2048] = x_prompt[c]
        g0 = 2048 * c - 2 * H
        lo, hi = max(g0, 0), min(g0 + 2048 + 4 * H, S)
        x0[2048 + (lo - g0):2048 + (hi - g0)] = x_sample[0, lo:hi]
        rc = np.zeros((2, 32, NT0), np.float32)
        rs = np.zeros((2, 32, NT0), np.float32)
        kvd = np.zeros((2, 128, NBLK), np.float32)
        pos0 = np.concatenate([np.arange(2048), g0 + np.arange(2048 + 4 * H)])
        val0 = np.concatenate([np.ones(2048), ((g0 + np.arange(2048 + 4 * H) >= 0) & (g0 + np.arange(2048 + 4 * H) < S))])
        g1 = 2048 * c - H
        pos1 = np.concatenate([np.arange(2048), g1 + np.arange(2048 + 2 * H)])
        val1 = np.concatenate([np.ones(2048), ((g1 + np.arange(2048 + 2 * H) >= 0) & (g1 + np.arange(2048 + 2 * H) < S))])
        for li, (pos, val) in enumerate(((pos0, val0), (pos1, val1))):
            cc, ss = rope_tables(pos)
            rc[li, :, :len(pos)] = cc
            rs[li, :, :len(pos)] = ss
            kvd[li, :, :len(pos) // 128] = val.reshape(-1, 128)[:, 0][None, :]
        m = dict(shared)
        m.update(x0=x0, ropec=rc, ropes=rs, kvalid=kvd)
        in_maps.append(m)
    if "nc" not in _CACHE:
        _CACHE["nc"] = build_program(cfg)
    res = run_bass_kernel_spmd(_CACHE["nc"], in_maps, core_ids=list(range(8)))
    yp = np.stack([res.results[c]["yp"] for c in range(8)], 0).astype(np.float32)
    ys = np.concatenate([res.results[c]["ys"] for c in range(8)], 0)[None].astype(np.float32)
    return yp, ys
```

```python
import contextlib
import numpy as np
import concourse.bass as bass
import concourse.mybir as mybir
from concourse.bass_utils import run_bass_kernel_spmd

F32 = mybir.dt.float32
BF16 = mybir.dt.bfloat16
AF = mybir.ActivationFunctionType
ALU = mybir.AluOpType
AX = mybir.AxisListType

D = 4096
NIN = 24096
HALO = 256
ALL_SP = True
ROPE_THETA = 500000.0

SPL = dict(a_q=(0, 768), a_k=(768, 768), a_v=(1536, 1536), a_lr=(3072, 32), a_gate=(3104, 1536),
           b_u=(4640, 1024), b_v=(5664, 1024), b_gate=(6688, 1024), c_q=(7712, 1536), c_k=(9248, 512),
           c_v=(9760, 512), c_gate=(10272, 1536), g=(11808, 12288))
FM_GROUPS = ["a_q", "a_k", "a_lr", "a_gate", "b_u", "b_gate", "c_q", "c_k", "c_gate", "g"]
TM_GROUPS = ["a_v", "b_v", "c_v"]
FM_OFF = dict(a_q=0, a_k=768, a_gate=1536, b_u=3072, b_gate=4096, c_q=5120, c_k=6656, c_gate=7168, g=8704)
FM_ROWS = 8704 + 12288
TM_OFF = dict(a_k=0, a_v=768, b_v=2304, c_v=3328)
TM_COLS = 3840
TMW = 256


def fm_blocks():
    out = []
    for g in FM_GROUPS:
        s, w = SPL[g]
        nb = max(1, w // 128)
        for b in range(nb):
            out.append((g, b, s + b * 128, min(128, w)))
    return out


def tm_blocks():
    out = []
    for g in TM_GROUPS:
        s, w = SPL[g]
        for b in range(w // TMW):
            out.append((g, b, s + b * TMW, TMW))
    return out


FMB = fm_blocks()
TMB = tm_blocks()
NFMB = len(FMB) + 32
NTMB = len(TMB) + 16


class _Stop(Exception):
    pass


class Buf:
    __slots__ = ("name", "w", "r", "sem", "semval", "queue", "rd")

    def __init__(self, name):
        self.name = name
        self.w = None
        self.r = {}
        self.sem = None
        self.semval = 0
        self.queue = None
        self.rd = None


class Ring:
    def __init__(self, slots, load_fn, ntasks):
        self.slots = slots
        self.load_fn = load_fn
        self.n = ntasks
        self.next = 0
        for _ in range(len(slots)):
            self.issue()

    def issue(self):
        if self.next < self.n:
            i = self.next
            t, b = self.slots[i % len(self.slots)]
            self.load_fn(i, t, b)
            self.next += 1

    def get(self, i):
        return self.slots[i % len(self.slots)]

    def done(self, i):
        self.issue()


class K:
    EPOCH = 40000
    CE = ("pe", "act", "dve", "pool")

    def __init__(self, nc, es):
        self.nc = nc
        self.es = es
        self.e = dict(pe=nc.tensor, act=nc.scalar, dve=nc.vector, pool=nc.gpsimd, sp=nc.sync)
        self.seq = {e: 0 for e in self.CE}
        self.sems = {e: [] for e in self.CE}
        self.known = {e: {} for e in self.e}
        self.dmabufs = []
        self.allbufs = []
        self.free = []
        self.nsem = 0
        self.ninst = 0
        self.round = 0
        self.ntile = 0
        self.bar_a = es.enter_context(nc.semaphore("bar_a"))
        self.bar_b = es.enter_context(nc.semaphore("bar_b"))

    def buf(self, name):
        b = Buf(name)
        self.allbufs.append(b)
        return b

    def _sem(self, name):
        if self.free:
            return self.free.pop()
        self.nsem += 1
        return self.es.enter_context(self.nc.semaphore("s%d" % self.nsem))

    def esem(self, e, n):
        ep = (n - 1) // self.EPOCH
        while len(self.sems[e]) <= ep:
            self.sems[e].append(self._sem("s_%s_%d" % (e, len(self.sems[e]))))
        return self.sems[e][ep], n - ep * self.EPOCH

    def _wait(self, waiter, dep):
        if dep is None:
            return
        kn = self.known[waiter]
        if dep[0] == "e":
            _, f, n = dep
            if f == waiter and f == "pe":
                return
            if kn.get(f, 0) >= n:
                return
            sem, val = self.esem(f, n)
            self.e[waiter].wait_ge(sem, val)
            kn[f] = n
        else:
            _, buf, val = dep
            key = ("d", id(buf))
            if kn.get(key, 0) >= val:
                return
            self.e[waiter].wait_ge(buf.sem, val)
            kn[key] = val
        self.ninst += 1

    def _deps(self, r, w):
        deps = []
        for b in r:
            deps.append(b.w)
        for b in w:
            deps.append(b.w)
            deps.append(b.rd)
            for f, n in b.r.items():
                deps.append(("e", f, n))
        return deps

    def I(self, eng, fn, r=(), w=()):
        for d in self._deps(r, w):
            self._wait(eng, d)
        ins = fn()
        self.seq[eng] += 1
        n = self.seq[eng]
        sem, _ = self.esem(eng, n)
        ins.then_inc(sem, 1)
        self.ninst += 1
        for b in r:
            if b.r.get(eng, 0) < n:
                b.r[eng] = n
        for b in w:
            b.w = ("e", eng, n)
            b.r = {}
            b.rd = None
        return ins

    def dma(self, q, out, in_, r=(), w=(), **kw):
        if ALL_SP:
            q = "sp"
        for d in self._deps(r, w):
            self._wait(q, d)
        tr = list(r) + list(w)
        assert len(tr) == 1, "one tracked SBUF buf per DMA"
        b = tr[0]
        if b.sem is None:
            b.sem = self._sem("sd_" + b.name)
            b.queue = q
            b.semval = 0
            self.dmabufs.append(b)
        assert b.queue == q, (b.name, b.queue, q)
        ins = self.e[q].dma_start(out=out, in_=in_, **kw)
        b.semval += 16
        assert b.semval < 65000, b.name
        ins.then_inc(b.sem, 16)
        self.ninst += 1
        if w:
            b.w = ("d", b, b.semval)
            b.r = {}
            b.rd = None
        else:
            b.rd = ("d", b, b.semval)
        return ins

    def barrier(self):
        for e in self.e:
            for f in self.CE:
                if f != e and self.seq[f] > 0:
                    self._wait(e, ("e", f, self.seq[f]))
            for b in self.dmabufs:
                if b.semval > 0:
                    self._wait(e, ("d", b, b.semval))

    def reset(self):
        self.barrier()
        self.round += 1
        pool = self.e["pool"]
        if self.seq["pool"] > 0:
            sem, val = self.esem("pool", self.seq["pool"])
            pool.wait_ge(sem, val)
        for e in self.e:
            self.e[e].sem_inc(self.bar_a, 1)
        pool.wait_ge(self.bar_a, 5 * self.round)
        used = []
        for e in self.CE:
            used += self.sems[e]
            self.sems[e] = []
        for b in self.dmabufs:
            used.append(b.sem)
            b.sem = None
            b.semval = 0
            b.queue = None
        self.dmabufs = []
        for sm in used:
            pool.sem_clear(sm)
        pool.sem_inc(self.bar_b, 1)
        for e in self.e:
            self.e[e].wait_ge(self.bar_b, self.round)
        self.free += used
        for b in self.allbufs:
            b.w = None
            b.r = {}
            b.rd = None
        self.seq = {e: 0 for e in self.CE}
        self.known = {e: {} for e in self.e}
        self.ninst += 20 + len(used)


def build_program(cfg):
    nc = bass.Bass("TRN2", target_bir_lowering=False)
    depth = cfg["depth"]
    NTM = max(L["NT"] for L in cfg["layers"])
    NBLK = NTM // 128

    def din(name, shape, dt=F32):
        return nc.dram_tensor(name, list(shape), dt, kind="ExternalInput").ap()

    x0 = din("x0", [cfg["layers"][0]["NT"], D])
    norm_gain = din("norm_gain", [depth, D])
    w_in = din("w_in", [depth, D, NIN])
    gate_up = din("gla_gate_up", [depth, 2, 16, 768])
    gate_b = din("gla_gate_bias", [depth, 2, 768])
    gla_ng = din("gla_norm_gain", [depth, 256])
    ln_g = din("sgu_ln_gain", [depth, 1024])
    ln_b = din("sgu_ln_bias", [depth, 1024])
    sgu_w = din("sgu_w", [depth, 8, 128, 128])
    sgu_b = din("sgu_b", [depth, 8, 128])
    qng = din("q_norm_gain", [depth, 128])
    kng = din("k_norm_gain", [depth, 128])
    sink = din("sink", [depth, 12])
    gbias = din("gate_bias", [depth, 3, D])
    w_br = din("w_br", [depth, D, D])
    w_out = din("w_out", [depth, D, D])
    cst = din("cst", [128, 6 * 128])
    ropec = din("ropec", [depth, 32, NTM])
    ropes = din("ropes", [depth, 32, NTM])
    kvalid = din("kvalid", [depth, 128, NBLK])
    youts = []
    for nm, n in cfg["outs"]:
        youts.append(nc.dram_tensor(nm, [n, D], F32, kind="ExternalOutput").ap())
    outmap = {nm: ap for (nm, _), ap in zip(cfg["outs"], youts)}

    dbg = cfg.get("debug", False)
    stop = cfg.get("stop", None)
    skind = "ExternalOutput" if dbg else "Internal"
    FMW = [nc.dram_tensor("FMW%d" % i, [NFMB, 128, 4096], BF16, kind=skind).ap() for i in range(depth)]
    TMWs = [nc.dram_tensor("TMWs%d" % i, [NTMB, 128, 32 * TMW], BF16, kind=skind).ap() for i in range(depth)]
    FM = nc.dram_tensor("FM", [FM_ROWS, NTM], BF16, kind=skind).ap()
    LRT = nc.dram_tensor("LRT", [32, NTM], F32, kind=skind).ap()
    TM = nc.dram_tensor("TM", [NTM, TM_COLS], BF16, kind=skind).ap()
    OFM = nc.dram_tensor("OFM", [4096, NTM], BF16, kind=skind).ap()
    OFW = nc.dram_tensor("OFW", [1536, NTM], F32, kind=skind).ap()
    X1 = nc.dram_tensor("X1", [NTM, D], F32, kind=skind).ap()
    outmap["x1"] = X1

    with contextlib.ExitStack() as es:
        k = K(nc, es)
        E = es.enter_context

        def sb(name, shape, dt=F32, stack=None):
            k.ntile += 1
            name = "%s_%d" % (name, k.ntile)
            t = (stack or es).enter_context(nc.sbuf_tensor(name, list(shape), dt))
            return t, k.buf(name)

        PS = E(nc.psum_tensor("PS", [128, 8 * 512], F32))
        psb = [k.buf("ps%d" % i) for i in range(8)]
        pstate = {"i": 0}

        def ps(n=1):
            i = pstate["i"]
            if i + n > 8:
                i = 0
            pstate["i"] = (i + n) % 8
            return PS[:, i * 512:(i + n) * 512], psb[i:i + n]

        cst_f, cst_fb = sb("cst_f", [128, 768])
        cst_h, cst_hb = sb("cst_h", [128, 768], BF16)
        ones_h, ones_hb = sb("ones_h", [128, 128], BF16)
        k.dma("sp", cst_f[:], cst, w=[cst_fb])
        k.I("dve", lambda: nc.vector.tensor_copy(out=cst_h[:], in_=cst_f[:]), r=[cst_fb], w=[cst_hb])
        k.I("pool", lambda: nc.gpsimd.memset(ones_h[:], 1.0), w=[ones_hb])
        epsq, epsqb = sb("epsq", [128, 4], F32)
        k.I("pool", lambda: nc.gpsimd.memset(epsq[:, 0:1], 128e-6), w=[epsqb])
        k.I("pool", lambda: nc.gpsimd.memset(epsq[:, 1:2], 1e-5), w=[epsqb])
        k.I("pool", lambda: nc.gpsimd.memset(epsq[:, 2:3], 1e-6), w=[epsqb])
        k.I("pool", lambda: nc.gpsimd.memset(epsq[:, 3:4], 1.0), w=[epsqb])
        ident_h = cst_h[:, 0:128]
        ident_f = cst_f[:, 0:128]
        U_f = cst_f[:, 128:256]
        L_f = cst_f[:, 256:384]
        Un_f = cst_f[:, 384:512]
        Ln_f = cst_f[:, 512:640]
        rot_h = cst_h[:, 640:768]

        with contextlib.ExitStack() as st:
            NS = 4
            stg = [sb("cv_s%d" % i, [128, 32, 256], F32, st) for i in range(NS)]
            cvo = [sb("cv_o%d" % i, [128, 32 * 256], BF16, st) for i in range(NS)]
            ceng = ["dve", "pool", "act"]
            ctasks = []
            for l in range(1):
                j = 0
                while j < len(FMB):
                    g, b, c0, wd = FMB[j]
                    if wd == 128 and j + 1 < len(FMB) and FMB[j + 1][0] == g:
                        src = w_in[l, :, c0:c0 + 256].rearrange("(k p) c -> p k c", p=128)
                        ctasks.append((src, 256, [(FMW[l][j].rearrange("p (k c) -> p k c", c=128), 0, 128),
                                                  (FMW[l][j + 1].rearrange("p (k c) -> p k c", c=128), 128, 128)]))
                        j += 2
                    else:
                        src = w_in[l, :, c0:c0 + wd].rearrange("(k p) c -> p k c", p=128)
                        ctasks.append((src, wd, [(FMW[l][j, :, 0:32 * wd].rearrange("p (k c) -> p k c", c=wd), 0, wd)]))
                        j += 1
                for f in range(0, 32, 2):
                    src = w_br[l, :, f * 128:(f + 2) * 128].rearrange("(k p) c -> p k c", p=128)
                    ctasks.append((src, 256, [(FMW[l][len(FMB) + f].rearrange("p (k c) -> p k c", c=128), 0, 128),
                                              (FMW[l][len(FMB) + f + 1].rearrange("p (k c) -> p k c", c=128), 128, 128)]))
                for j, (g, b, c0, wd) in enumerate(TMB):
                    src = w_in[l, :, c0:c0 + TMW].rearrange("(k p) c -> p k c", p=128)
                    ctasks.append((src, TMW, [(TMWs[l][j].rearrange("p (k c) -> p k c", c=TMW), 0, TMW)]))
                for f in range(16):
                    src = w_out[l, :, f * TMW:(f + 1) * TMW].rearrange("(k p) c -> p k c", p=128)
                    ctasks.append((src, TMW, [(TMWs[l][len(TMB) + f].rearrange("p (k c) -> p k c", c=TMW), 0, TMW)]))

            def cload(i, t, b):
                src, W, dsts = ctasks[i]
                k.dma("sp", t[:, :, 0:W], src, w=[b])

            cring = Ring(stg, cload, len(ctasks))
            ci = 0
            for i, (src, W, dsts) in enumerate(ctasks):
                s_, sbf = cring.get(i)
                o, obf = cvo[i % NS]
                off = 0
                views = []
                for (dst, c0, w) in dsts:
                    ov = o[:, off:off + 32 * w].rearrange("p (k c) -> p k c", c=w)
                    off += 32 * w
                    e = ceng[ci % 3]
                    ci += 1
                    if e == "act":
                        k.I("act", lambda: nc.scalar.copy(out=ov, in_=s_[:, :, c0:c0 + w]), r=[sbf], w=[obf])
                    elif e == "dve":
                        k.I("dve", lambda: nc.vector.tensor_copy(out=ov, in_=s_[:, :, c0:c0 + w]), r=[sbf], w=[obf])
                    else:
                        k.I("pool", lambda: nc.gpsimd.tensor_copy(out=ov, in_=s_[:, :, c0:c0 + w]), r=[sbf], w=[obf])
                    views.append((dst, ov))
                cring.done(i)
                for (dst, ov) in views:
                    k.dma("sp", dst, ov, r=[obf])
            k.reset()

        def chk(tag):
            return stop == tag

        def do_layer(l):
            L = cfg["layers"][l]
            NT = L["NT"]
            xsrc = x0 if L["xsrc"] == "x0" else X1
            with contextlib.ExitStack() as lst:
                gb_t, gb_tb = sb("gb_t", [128, 96], F32, lst)
                gb_r, gb_rb = sb("gb_r", [96, 128], F32, lst)
                k.dma("sp", gb_r[:], gbias[l].rearrange("i (b p) -> (i b) p", p=128), w=[gb_rb])
                pg_, pgb_ = ps(1)
                k.I("pe", lambda: nc.tensor.matmul(pg_[:, 0:96], lhsT=gb_r[:], rhs=cst_f[0:96, 0:96], start=True, stop=True),
                    r=[gb_rb, cst_fb], w=pgb_)
                k.I("dve", lambda: nc.vector.tensor_copy(out=gb_t[:], in_=pg_[:, 0:96]), r=pgb_, w=[gb_tb])
                qkg, qkgb = sb("qkg", [128, 2], F32, lst)
                k.dma("sp", qkg[:, 0:1], qng[l].rearrange("(p o) -> p o", o=1), w=[qkgb])
                k.dma("sp", qkg[:, 1:2], kng[l].rearrange("(p o) -> p o", o=1), w=[qkgb])
                k.I("dve", lambda: nc.vector.tensor_scalar_mul(out=qkg[:], in0=qkg[:], scalar1=float(np.sqrt(128.0))),
                    r=[qkgb], w=[qkgb])

                if chk("p1a"):

                    return True
                with contextlib.ExitStack() as st:
                    gain_bc, gain_bcb = sb("gain_bc", [128, D], F32, st)
                    k.dma("sp", gain_bc[:], norm_gain[l:l + 1, :].to_broadcast([128, D]), w=[gain_bcb])
                    xnT, xnTb = sb("xnT", [128, 32, 512], BF16, st)
                    xld = [sb("xld%d" % i, [128, D], F32, st) for i in range(1)]
                    xnb = [sb("xnb%d" % i, [128, D], BF16, st) for i in range(1)]
                    junk, junkb = sb("junk", [128, D], BF16, st)
                    stat, statb = sb("stat", [128, 8], F32, st)
                    wf = [sb("wf%d" % i, [128, 32, 128], BF16, st) for i in range(3)]
                    wt = [sb("wt%d" % i, [128, 32, TMW], BF16, st) for i in range(2)]
                    P1T = L.get("p1")
                    if P1T is None:
                        P1T = [(t_, 512, "full") for t_ in range(0, NT, 512)]
                    HALO_FM = ("a_k", "a_lr", "c_k")
                    HALO_TM = ("a_v", "c_v")
                    ftl, ttl = [], []
                    tile_f, tile_t = [], []
                    for (t_, T_, md) in P1T:
                        fj = [j for j, blk_ in enumerate(FMB) if md == "full" or blk_[0] in HALO_FM]
                        tj = [j for j, blk_ in enumerate(TMB) if md == "full" or blk_[0] in HALO_TM]
                        tile_f.append((len(ftl), fj))
                        tile_t.append((len(ttl), tj))
                        ftl += fj
                        ttl += tj

                    def fload(i, t, b):
                        j = ftl[i]
                        wd = FMB[j][3]
                        k.dma("sp", t[:].rearrange("p k c -> p (k c)")[:, 0:32 * wd], FMW[l][j, :, 0:32 * wd], w=[b])

                    def tload(i, t, b):
                        k.dma("sp", t[:].rearrange("p k c -> p (k c)"), TMWs[l][ttl[i]], w=[b])

                    fring = Ring(wf, fload, len(ftl))
                    tring = Ring(wt, tload, len(ttl))
                    qtasks = []
                    if l + 1 < depth:
                        ln_ = l + 1
                        for j, (g, b, c0, wd) in enumerate(FMB):
                            for kq in range(4):
                                src = w_in[ln_, kq * 1024:(kq + 1) * 1024, c0:c0 + wd].rearrange("(k p) c -> p k c", p=128)
                                dstv = FMW[ln_][j, :, 0:32 * wd].rearrange("p (k c) -> p k c", c=wd)[:, kq * 8:(kq + 1) * 8, :]
                                qtasks.append((src, dstv, wd))
                        for f in range(32):
                            for kq in range(4):
                                src = w_br[ln_, kq * 1024:(kq + 1) * 1024, f * 128:(f + 1) * 128].rearrange("(k p) c -> p k c", p=128)
                                dstv = FMW[ln_][len(FMB) + f].rearrange("p (k c) -> p k c", c=128)[:, kq * 8:(kq + 1) * 8, :]
                                qtasks.append((src, dstv, 128))
                        for j, (g, b, c0, wd) in enumerate(TMB):
                            for hh in range(2):
                                for kq in range(4):
                                    src = w_in[ln_, kq * 1024:(kq + 1) * 1024, c0 + hh * 128:c0 + (hh + 1) * 128].rearrange("(k p) c -> p k c", p=128)
                                    dstv = TMWs[ln_][j].rearrange("p (k c) -> p k c", c=TMW)[:, kq * 8:(kq + 1) * 8, hh * 128:(hh + 1) * 128]
                                    qtasks.append((src, dstv, 128))
                        for f in range(16):
                            for hh in range(2):
                                for kq in range(4):
                                    cc0 = f * TMW + hh * 128
                                    src = w_out[ln_, kq * 1024:(kq + 1) * 1024, cc0:cc0 + 128].rearrange("(k p) c -> p k c", p=128)
                                    dstv = TMWs[ln_][len(TMB) + f].rearrange("p (k c) -> p k c", c=TMW)[:, kq * 8:(kq + 1) * 8, hh * 128:(hh + 1) * 128]
                                    qtasks.append((src, dstv, 128))
                    qst = [sb("q_s%d" % i, [128, 8, 128], F32, st) for i in range(2)]
                    qot = [sb("q_o%d" % i, [128, 8, 128], BF16, st) for i in range(2)]
                    qring = Ring(qst, lambda i, t, b: k.dma("sp", t[:, :, 0:qtasks[i][2]], qtasks[i][0], w=[b]), len(qtasks))
                    qn = [0]

                    def conv_step():
                        i = qn[0]
                        if i >= len(qtasks):
                            return False
                        qn[0] += 1
                        src, dstv, wd = qtasks[i]
                        s_, sbf = qring.get(i)
                        o_, obf = qot[i % 2]
                        k.I("pool", lambda: nc.gpsimd.tensor_copy(out=o_[:, :, 0:wd], in_=s_[:, :, 0:wd]), r=[sbf], w=[obf])
                        qring.done(i)
                        k.dma("sp", dstv, o_[:, :, 0:wd], r=[obf])
                        return True

                    deferred = []
                    blkno = [0]
                    ofm = [sb("ofm%d" % i, [128, 512], BF16, st) for i in range(6)]
                    otm = [sb("otm%d" % i, [128, 4, TMW], BF16, st) for i in range(2)]
                    lrs, lrsb = sb("lrs", [32, 512], F32, st)
                    G, Gb = sb("G", [128, 4, 1024], F32, st)
                    vvo, vvob = junk[:].rearrange("p (a b) -> p a b", b=1024), junkb
                    lnj, lnjb = sb("lnj", [128, 1024], BF16, st)
                    lng, lngb = sb("lng", [128, 1024], F32, st)
                    lnb, lnbb = sb("lnb", [128, 1024], F32, st)
                    k.dma("sp", lng[:], ln_g[l:l + 1, :].to_broadcast([128, 1024]), w=[lngb])
                    k.dma("sp", lnb[:], ln_b[l:l + 1, :].to_broadcast([128, 1024]), w=[lnbb])
                    rc, rcb = sb("rc", [32, 512], F32, st)
                    rs, rsb = sb("rs", [32, 512], F32, st)
                    sq, sqb = sb("sq", [128, 512], BF16, st)
                    rstd, rstdb = sb("rstd", [128, 512], F32, st)
                    t1, t1b = sb("t1", [32, 512], F32, st)
                    t2, t2b = sb("t2", [32, 512], F32, st)
                    lnst, lnstb = sb("lnst", [128, 8], F32, st)
                    cn = dict(wf=0, wt=0, ofm=0, otm=0, akt=0)
                    akt = [sb("akt%d" % i, [128, 512], BF16, st) for i in range(2)]

                    for ti, (tok0, T, tmode) in enumerate(P1T):
                        NSUB = T // 128
                        fbase, fjl = tile_f[ti]
                        tbase, tjl = tile_t[ti]
                        k.dma("sp", rc[:, 0:T], ropec[l, :, tok0:tok0 + T], w=[rcb])
                        k.dma("sp", rs[:, 0:T], ropes[l, :, tok0:tok0 + T], w=[rsb])
                        for sub in range(NSUB):
                            (xl, xlb), (xn, xnbb) = xld[0], xnb[0]
                            r0 = tok0 + sub * 128
                            k.dma("sp", xl[:], xsrc[r0:r0 + 128, :], w=[xlb])
                            k.I("act", lambda: nc.scalar.activation(out=xn[:], in_=xl[:], func=AF.Square), r=[xlb], w=[xnbb])
                            k.I("dve", lambda: nc.vector.reduce_sum(out=stat[:, sub:sub + 1], in_=xn[:], axis=AX.X), r=[xnbb], w=[statb])
                            k.I("dve", lambda: nc.vector.tensor_scalar(out=stat[:, 4 + sub:5 + sub], in0=stat[:, sub:sub + 1],
                                                                       scalar1=1.0 / D, scalar2=1e-6, op0=ALU.mult, op1=ALU.add),
                                r=[statb], w=[statb])
                            k.I("act", lambda: nc.scalar.activation(out=stat[:, 4 + sub:5 + sub], in_=stat[:, 4 + sub:5 + sub], func=AF.Sqrt),
                                r=[statb], w=[statb])
                            k.I("dve", lambda: nc.vector.reciprocal(out=stat[:, 4 + sub:5 + sub], in_=stat[:, 4 + sub:5 + sub]),
                                r=[statb], w=[statb])
                            k.I("dve", lambda: nc.vector.scalar_tensor_tensor(out=xn[:], in0=xl[:], scalar=stat[:, 4 + sub:5 + sub],
                                                                              in1=gain_bc[:], op0=ALU.mult, op1=ALU.mult),
                                r=[xlb, statb, gain_bcb], w=[xnbb])
                            for kg in range(8):
                                pa, pb = ps(1)
                                for q4 in range(4):
                                    kc = kg * 4 + q4
                                    k.I("pe", lambda: nc.tensor.matmul(pa[:, q4 * 128:(q4 + 1) * 128], lhsT=xn[:, kc * 128:(kc + 1) * 128],
                                                                       rhs=ident_h, start=True, stop=True),
                                        r=[xnbb, cst_hb], w=pb)
                                dst = xnT[:, kg * 4:(kg + 1) * 4, sub * 128:(sub + 1) * 128]
                                src = pa.rearrange("p (a b) -> p a b", b=128)
                                if kg % 2 == 0:
                                    k.I("act", lambda: nc.scalar.copy(out=dst, in_=src), r=pb, w=[xnTb])
                                else:
                                    k.I("dve", lambda: nc.vector.tensor_copy(out=dst, in_=src), r=pb, w=[xnTb])

                        if chk("p1b"):

                            return True
                        for fi_, j in enumerate(fjl):
                            g, b, c0, wd = FMB[j]
                            if chk("p1c%d" % j):
                                return True
                            wtile, wb_ = fring.get(fbase + fi_)
                            wv = wtile[:].rearrange("p k c -> p (k c)")[:, 0:32 * wd].rearrange("p (k c) -> p k c", c=wd)
                            pa, pb = ps(1)
                            for kc in range(32):
                                k.I("pe", lambda: nc.tensor.matmul(pa[0:wd, 0:T], lhsT=wv[:, kc, :], rhs=xnT[:, kc, 0:T],
                                                                   start=(kc == 0), stop=(kc == 31)),
                                    r=[wb_, xnTb], w=pb)
                            fring.done(fbase + fi_)
                            blkno[0] += 1
                            while deferred and deferred[0][0] <= blkno[0]:
                                deferred.pop(0)[1]()
                            conv_step()
                            if g == "a_lr":
                                k.I("act", lambda: nc.scalar.copy(out=lrs[:, 0:T], in_=pa[0:32, 0:T]), r=pb, w=[lrsb])
                                k.dma("pool", LRT[:, tok0:tok0 + T], lrs[:, 0:T], r=[lrsb])
                                continue
                            o, ob = ofm[cn["ofm"] % 6]
                            cn["ofm"] += 1
                            if g == "a_q":
                                k.I("act", lambda: nc.scalar.activation(out=o[:, 0:T], in_=pa[:, 0:T], func=AF.Copy, scale=float(128.0 ** -0.5)),
                                    r=pb, w=[ob])
                            elif g == "a_k":
                                k.I("dve", lambda: nc.vector.tensor_copy(out=o[:, 0:T], in_=pa[:, 0:T]), r=pb, w=[ob])

                                def CT(o=o, ob=ob, b=b, tok0=tok0, T=T, NSUB=NSUB):
                                    pt, ptb = ps(1)
                                    for sub_ in range(NSUB):
                                        k.I("pe", lambda: nc.tensor.matmul(pt[:, sub_ * 128:(sub_ + 1) * 128], lhsT=o[:, sub_ * 128:(sub_ + 1) * 128],
                                                                           rhs=ident_h, start=True, stop=True), r=[ob, cst_hb], w=ptb)
                                    at_, atb_ = akt[cn["akt"] % 2]
                                    cn["akt"] += 1
                                    k.I("act", lambda: nc.scalar.copy(out=at_[:, 0:T], in_=pt[:, 0:T]), r=ptb, w=[atb_])
                                    k.dma("sp", TM[tok0:tok0 + T, b * 128:(b + 1) * 128].rearrange("(s p) c -> p s c", p=128),
                                          at_[:, 0:T].rearrange("p (s c) -> p s c", c=128), r=[atb_])
                                deferred.append((blkno[0] + 1, CT))
                                deferred.sort(key=lambda x_: x_[0])
                            elif g in ("a_gate", "b_gate", "c_gate"):
                                k.I("act", lambda: nc.scalar.activation(out=o[:, 0:T], in_=pa[:, 0:T], func=AF.Silu), r=pb, w=[ob])
                            elif g == "b_u":
                                k.I("act", lambda: nc.scalar.activation(out=o[:, 0:T], in_=pa[:, 0:T], func=AF.Gelu_apprx_tanh), r=pb, w=[ob])
                            elif g == "g":
                                gi = b
                                k.I("act", lambda: nc.scalar.activation(out=o[:, 0:T], in_=pa[:, 0:T], func=AF.Sigmoid, bias=gb_t[:, gi:gi + 1]),
                                    r=pb + [gb_tb], w=[ob])
                            else:
                                gcol = 0 if g == "c_q" else 1
                                k.I("act", lambda: nc.scalar.activation(out=sq[:, 0:T], in_=pa[:, 0:T], func=AF.Square), r=pb, w=[sqb])

                                def C1(pa=pa, pb=pb, o=o, ob=ob, gcol=gcol, row=FM_OFF[g] + b * 128, tok0=tok0, T=T):
                                    pq, pqb = ps(1)
                                    k.I("pe", lambda: nc.tensor.matmul(pq[:, 0:T], lhsT=ones_h[:], rhs=sq[:, 0:T], start=True, stop=True),
                                        r=[ones_hb, sqb], w=pqb)
                                    k.I("act", lambda: nc.scalar.activation(out=rstd[:, 0:T], in_=pq[:, 0:T], func=AF.Sqrt, bias=epsq[:, 0:1]), r=pqb + [epsqb], w=[rstdb])
                                    k.I("dve", lambda: nc.vector.reciprocal(out=rstd[:, 0:T], in_=rstd[:, 0:T]), r=[rstdb], w=[rstdb])
                                    k.I("dve", lambda: nc.vector.scalar_tensor_tensor(out=o[:, 0:T], in0=pa[:, 0:T], scalar=qkg[:, gcol:gcol + 1], in1=rstd[:, 0:T],
                                                                                      op0=ALU.mult, op1=ALU.mult),
                                        r=pb + [qkgb, rstdb], w=[ob])

                                    def C2():
                                        pr, prb = ps(1)
                                        k.I("pe", lambda: nc.tensor.matmul(pr[0:32, 0:T], lhsT=rot_h[:, 0:32], rhs=o[:, 0:T], start=True, stop=True),
                                            r=[cst_hb, ob], w=prb)
                                        k.I("dve", lambda: nc.vector.tensor_tensor(out=t1[:, 0:T], in0=pr[0:32, 0:T], in1=rs[:, 0:T], op=ALU.mult),
                                            r=prb + [rsb], w=[t1b])
                                        k.I("pool", lambda: nc.gpsimd.tensor_tensor(out=t2[:, 0:T], in0=o[0:32, 0:T], in1=rc[:, 0:T], op=ALU.mult),
                                            r=[ob, rcb], w=[t2b])
                                        k.I("dve", lambda: nc.vector.tensor_tensor(out=o[0:32, 0:T], in0=t1[:, 0:T], in1=t2[:, 0:T], op=ALU.add),
                                            r=[t1b, t2b], w=[ob])
                                        k.dma("sp", FM[row:row + 128, tok0:tok0 + T], o[:, 0:T], r=[ob])
                                    deferred.append((blkno[0] + 2, C2))
                                    deferred.sort(key=lambda x_: x_[0])
                                deferred.append((blkno[0] + 1, C1))
                                deferred.sort(key=lambda x_: x_[0])
                                continue
                            row = FM_OFF[g] + b * 128
                            k.dma("pool", FM[row:row + 128, tok0:tok0 + T], o[:, 0:T], r=[ob])

                        while deferred:
                            deferred.pop(0)[1]()
                        if chk("p1d"):
                            return True
                        for tq_, j in enumerate(tjl):
                            g, b, c0, wd = TMB[j]
                            if chk("p1e%d" % j):
                                return True
                            wtile, wb_ = tring.get(tbase + tq_)
                            o, ob = otm[cn["otm"] % 2]
                            cn["otm"] += 1
                            for s2 in range(NSUB // 2):
                                pa, pb = ps(1)
                                for s1 in range(2):
                                    sub = s2 * 2 + s1
                                    for kc in range(32):
                                        k.I("pe", lambda: nc.tensor.matmul(pa[:, s1 * TMW:(s1 + 1) * TMW], lhsT=xnT[:, kc, sub * 128:(sub + 1) * 128],
                                                                           rhs=wtile[:, kc, :], start=(kc == 0), stop=(kc == 31)),
                                            r=[wb_, xnTb], w=pb)
                                if s2 == NSUB // 2 - 1:
                                    tring.done(tbase + tq_)
                                    conv_step()
                                src = pa.rearrange("p (a b) -> p a b", b=TMW)
                                if g == "b_v":
                                    k.I("act", lambda: nc.scalar.activation(out=G[:, s2 * 2:s2 * 2 + 2, b * TMW:(b + 1) * TMW], in_=src,
                                                                            func=AF.Gelu_apprx_tanh), r=pb, w=[Gb])
                                else:
                                    k.I("dve", lambda: nc.vector.tensor_copy(out=o[:, s2 * 2:s2 * 2 + 2, :], in_=src), r=pb, w=[ob])
                            if g != "b_v":
                                col = TM_OFF[g] + b * TMW
                                k.dma("pool", TM[tok0:tok0 + T, col:col + TMW].rearrange("(s p) c -> p s c", p=128), o[:, 0:NSUB, :], r=[ob])
                            elif b == 3:
                                for sub in range(NSUB):
                                    k.I("dve", lambda: nc.vector.reduce_sum(out=lnst[:, 0:1], in_=G[:, sub, :], axis=AX.X), r=[Gb], w=[lnstb])
                                    k.I("act", lambda: nc.scalar.activation(out=lnj[:], in_=G[:, sub, :], func=AF.Square), r=[Gb], w=[lnjb])
                                    k.I("dve", lambda: nc.vector.reduce_sum(out=lnst[:, 1:2], in_=lnj[:], axis=AX.X), r=[lnjb], w=[lnstb])
                                    k.I("dve", lambda: nc.vector.tensor_scalar_mul(out=lnst[:, 2:3], in0=lnst[:, 0:1], scalar1=1.0 / 1024),
                                        r=[lnstb], w=[lnstb])
                                    k.I("dve", lambda: nc.vector.tensor_tensor(out=lnst[:, 3:4], in0=lnst[:, 2:3], in1=lnst[:, 2:3], op=ALU.mult),
                                        r=[lnstb], w=[lnstb])
                                    k.I("dve", lambda: nc.vector.scalar_tensor_tensor(out=lnst[:, 4:5], in0=lnst[:, 1:2], scalar=1.0 / 1024,
                                                                                      in1=lnst[:, 3:4], op0=ALU.mult, op1=ALU.subtract),
                                        r=[lnstb], w=[lnstb])
                                    k.I("act", lambda: nc.scalar.activation(out=lnst[:, 5:6], in_=lnst[:, 4:5], func=AF.Sqrt, bias=epsq[:, 1:2]),
                                        r=[lnstb, epsqb], w=[lnstb])
                                    k.I("dve", lambda: nc.vector.reciprocal(out=lnst[:, 5:6], in_=lnst[:, 5:6]), r=[lnstb], w=[lnstb])
                                    k.I("dve", lambda: nc.vector.tensor_scalar(out=G[:, sub, :], in0=G[:, sub, :], scalar1=lnst[:, 2:3],
                                                                               scalar2=lnst[:, 5:6], op0=ALU.subtract, op1=ALU.mult),
                                        r=[Gb, lnstb], w=[Gb])
                                    k.I("pool", lambda: nc.gpsimd.tensor_tensor(out=G[:, sub, :], in0=G[:, sub, :], in1=lng[:], op=ALU.mult),
                                        r=[Gb, lngb], w=[Gb])
                                    k.I("pool", lambda: nc.gpsimd.tensor_tensor(out=vvo[:, sub, :], in0=G[:, sub, :], in1=lnb[:], op=ALU.add),
                                        r=[Gb, lnbb], w=[vvob])
                                col = TM_OFF["b_v"]
                                k.dma("pool", TM[tok0:tok0 + T, col:col + 1024].rearrange("(s p) c -> p s c", p=128), vvo[:, 0:NSUB, :], r=[vvob])
                    while conv_step():
                        pass
                k.reset()

                if stop == "p1":
                    return True
                phase2(nc, k, sb, ps, l, L, dict(FM=FM, LRT=LRT, TM=TM, OFM=OFM, OFW=OFW, gate_up=gate_up, gate_b=gate_b,
                                                 gla_ng=gla_ng, sgu_w=sgu_w, sgu_b=sgu_b, sink=sink, kvalid=kvalid,
                                                 cst_f=cst_f, cst_fb=cst_fb, cst_h=cst_h, cst_hb=cst_hb,
                                                 ones_h=ones_h, ones_hb=ones_hb, epsq=epsq, epsqb=epsqb))
                k.reset()

                if stop == "p2":
                    return True
                with contextlib.ExitStack() as st:
                    Ots = [sb("Ot%d" % i, [128, 32, 512], BF16, st) for i in range(2)]
                    Mt, Mtb = sb("Mt", [128, 32, 512], BF16, st)
                    wf = [sb("p3wf%d" % i, [128, 32, 128], BF16, st) for i in range(4)]
                    wt = [sb("p3wt%d" % i, [128, 32, TMW], BF16, st) for i in range(2)]
                    gt = [sb("p3g%d" % i, [128, 3, 512], BF16, st) for i in range(3)]
                    m1 = [sb("p3m%d" % i, [128, 512], F32, st) for i in range(3)]
                    xr = [sb("p3x%d" % i, [128, 4, TMW], F32, st) for i in range(3)]
                    yo = [sb("p3y%d" % i, [128, 4, TMW], F32, st) for i in range(2)]
                    P3 = L["p3"]
                    np3 = len(P3)
                    g0_ = FM_OFF["g"]

                    def oload(i, t, b):
                        tk = P3[i][0]
                        k.dma("sp", t[:], OFM[:, tk:tk + 512].rearrange("(k p) t -> p k t", p=128), w=[b])

                    def wfload(i, t, b):
                        k.dma("sp", t[:].rearrange("p k c -> p (k c)"), FMW[l][len(FMB) + i % 32], w=[b])

                    def gload(i, t, b):
                        tk = P3[i // 32][0]
                        k.dma("sp", t[:], FM[g0_:g0_ + 3 * 4096, tk:tk + 512].rearrange("(i r p) t -> p i r t", i=3, p=128)[:, :, i % 32, :], w=[b])

                    def wtload(i, t, b):
                        k.dma("sp", t[:].rearrange("p k c -> p (k c)"), TMWs[l][len(TMB) + i % 16], w=[b])

                    def xload(i, t, b):
                        tk = P3[i // 16][0]
                        f2 = i % 16
                        k.dma("sp", t[:], xsrc[tk:tk + 512, f2 * TMW:(f2 + 1) * TMW].rearrange("(s p) c -> p s c", p=128), w=[b])

                    oring = Ring(Ots, oload, np3)
                    wfring = Ring(wf, wfload, np3 * 32)
                    gring = Ring(gt, gload, np3 * 32)
                    wtring = Ring(wt, wtload, np3 * 16)
                    xring = Ring(xr, xload, np3 * 16)
                    ycnt = 0
                    for pi, (tok0, dname, drow) in enumerate(P3):
                        dst = outmap[dname]
                        Ot, Otb = oring.get(pi)
                        for f in range(32):
                            wtile, wb_ = wfring.get(pi * 32 + f)
                            g3, g3b = gring.get(pi * 32 + f)
                            pss = []
                            for bi, (k0, k1) in enumerate(((0, 12), (12, 20), (20, 32))):
                                pa, pb = ps(1)
                                pss.append((pa, pb))
                                for kc in range(k0, k1):
                                    k.I("pe", lambda: nc.tensor.matmul(pa, lhsT=wtile[:, kc, :], rhs=Ot[:, kc, :],
                                                                       start=(kc == k0), stop=(kc == k1 - 1)),
                                        r=[wb_, Otb], w=pb)
                            wfring.done(pi * 32 + f)
                            if f == 31:
                                oring.done(pi)
                            for bi in range(3):
                                pa, pb = pss[bi]
                                mm, mmb = m1[bi]
                                k.I("dve", lambda: nc.vector.tensor_tensor(out=mm[:], in0=pa, in1=g3[:, bi, :], op=ALU.mult),
                                    r=pb + [g3b], w=[mmb])
                            gring.done(pi * 32 + f)
                            k.I("pool", lambda: nc.gpsimd.tensor_tensor(out=m1[0][0][:], in0=m1[0][0][:], in1=m1[1][0][:], op=ALU.add),
                                r=[m1[0][1], m1[1][1]], w=[m1[0][1]])
                            k.I("pool", lambda: nc.gpsimd.tensor_tensor(out=Mt[:, f, :], in0=m1[0][0][:], in1=m1[2][0][:], op=ALU.add),
                                r=[m1[0][1], m1[2][1]], w=[Mtb])
                        for f2 in range(16):
                            wtile, wb_ = wtring.get(pi * 16 + f2)
                            xt_, xtb = xring.get(pi * 16 + f2)
                            yt_, ytb = yo[ycnt % 2]
                            ycnt += 1
                            for s2 in range(2):
                                pa, pb = ps(1)
                                for s1 in range(2):
                                    sub = s2 * 2 + s1
                                    for kc in range(32):
                                        k.I("pe", lambda: nc.tensor.matmul(pa[:, s1 * TMW:(s1 + 1) * TMW], lhsT=Mt[:, kc, sub * 128:(sub + 1) * 128],
                                                                           rhs=wtile[:, kc, :], start=(kc == 0), stop=(kc == 31)),
                                            r=[wb_, Mtb], w=pb)
                                if s2 == 1:
                                    wtring.done(pi * 16 + f2)
                                k.I("dve", lambda: nc.vector.tensor_tensor(out=yt_[:, s2 * 2:s2 * 2 + 2, :], in0=pa.rearrange("p (a b) -> p a b", b=TMW),
                                                                           in1=xt_[:, s2 * 2:s2 * 2 + 2, :], op=ALU.add),
                                    r=pb + [xtb], w=[ytb])
                            xring.done(pi * 16 + f2)
                            k.dma("sp", dst[drow:drow + 512, f2 * TMW:(f2 + 1) * TMW].rearrange("(s p) c -> p s c", p=128), yt_[:], r=[ytb])
                k.reset()
            return False

        for l in range(depth if stop != "pre" else 0):
            if do_layer(l):
                break
        k.barrier()
        print("instructions:", k.ninst, "sems:", k.nsem, {e: k.seq[e] for e in k.seq})
    return nc


def phase2(nc, k, sb, ps, l, L, T):
    FM, LRT, TM, OFM, OFW = T["FM"], T["LRT"], T["TM"], T["OFM"], T["OFW"]
    cst_f, cst_fb, cst_h, cst_hb = T["cst_f"], T["cst_fb"], T["cst_h"], T["cst_hb"]
    ones_h, ones_hb = T["ones_h"], T["ones_hb"]
    epsq, epsqb = T["epsq"], T["epsqb"]
    ident_f = cst_f[:, 0:128]
    masks = {0: cst_f[:, 128:256], 1: cst_f[:, 256:384]}
    masks_h = {0: cst_h[:, 128:256], 1: cst_h[:, 256:384]}
    cums = {0: cst_f[:, 384:512], 1: cst_f[:, 512:640]}

    with contextlib.ExitStack() as st:
        Gaug = []
        for dr in range(2):
            t, tb = sb("Gaug%d" % dr, [33, 768], F32, st)
            k.I("dve", lambda: nc.vector.memset(t[:], 0.0), w=[tb])
            k.dma("sp", t[dr * 16:(dr + 1) * 16, :], T["gate_up"][l, dr], w=[tb])
            k.dma("sp", t[32:33, :], T["gate_b"][l, dr:dr + 1, :], w=[tb])
            Gaug.append((t, tb))
        gng, gngb = sb("gng", [128, 2], F32, st)
        k.dma("sp", gng[:], T["gla_ng"][l].rearrange("(j p) -> p j", p=128), w=[gngb], allow_slow_non_contiguous=True)
        qTs = [sb("g_qT%d" % i, [128, 6, 512], BF16, st) for i in range(2)]
        kTs = [sb("g_kT%d" % i, [128, 6, 512], BF16, st) for i in range(2)]
        lrs_ = [sb("g_lr%d" % i, [33, 512], F32, st) for i in range(2)]
        kvs = [sb("g_kv%d" % i, [128, 4, 2304], BF16, st) for i in range(2)]
        ag, agb = sb("g_ag", [128, 12, 512], BF16, st)
        of_, ofb = sb("g_of", [128, 12, 512], F32, st)
        ob_, obb = sb("g_ob", [128, 12, 512], F32, st)
        S, Sb = sb("g_S", [128, 6, 256], F32, st)
        Sh, Shb = sb("g_Sh", [128, 6, 256], BF16, st)
        tmpS, tmpSb = sb("g_tmpS", [128, 6, 256], F32, st)
        ez, ezb = sb("g_ez", [128, 768], F32, st)
        lz, lzb = sb("g_lz", [128, 768], F32, st)
        PR = []
        for i in range(2):
            d = {}
            for nm, dt_ in (("Ef", F32), ("Em", F32), ("Emt", F32), ("qs", BF16), ("ks", BF16), ("kst", BF16)):
                d[nm] = sb("g_%s%d" % (nm, i), [128, 768], dt_, st)
            PR.append(d)
        At, Atb = sb("g_At", [128, 768], BF16, st)
        sqh, sqhb = sb("g_sq", [128, 12, 512], BF16, st)
        rr, rrb = sb("g_rr", [128, 512], F32, st)
        for (t_, tb_) in lrs_:
            k.I("dve", lambda: nc.vector.memset(t_[:], 1.0), w=[tb_])

        for (s0, slen) in L["segs"]:
            nsc = slen // 512
            for dr in range(2):
                k.I("dve", lambda: nc.vector.memset(S[:], 0.0), w=[Sb])
                k.I("pool", lambda: nc.gpsimd.memset(Sh[:], 0.0), w=[Shb])
                last = 127 if dr == 0 else 0
                scs = list(range(nsc)) if dr == 0 else list(range(nsc - 1, -1, -1))
                chs = list(range(4)) if dr == 0 else list(range(3, -1, -1))

                def T0(i):
                    return s0 + scs[i] * 512

                rq = Ring(qTs, lambda i, t, b: k.dma("sp", t[:], FM[FM_OFF["a_q"]:FM_OFF["a_q"] + 768, T0(i):T0(i) + 512].rearrange("(h p) t -> p h t", p=128), w=[b]), nsc)
                rk = Ring(kTs, lambda i, t, b: k.dma("sp", t[:], FM[FM_OFF["a_k"]:FM_OFF["a_k"] + 768, T0(i):T0(i) + 512].rearrange("(h p) t -> p h t", p=128), w=[b]), nsc)
                rl = Ring(lrs_, lambda i, t, b: k.dma("sp", t[0:32, :], LRT[:, T0(i):T0(i) + 512], w=[b]), nsc)
                rv = Ring(kvs, lambda i, t, b: k.dma("sp", t[:], TM[T0(i):T0(i) + 512, 0:2304].rearrange("(c p) f -> p c f", p=128), w=[b]), nsc)
                items = [(i, ch) for i in range(nsc) for ch in chs]

                def prep(n):
                    i, ch = items[n]
                    P = PR[n % 2]
                    qT, qTb = rq.get(i)
                    kT, kTb = rk.get(i)
                    lr, lrb = rl.get(i)
                    kv, kvb = rv.get(i)
                    tsl = slice(ch * 128, (ch + 1) * 128)
                    pz, pzb = ps(2)
                    for hf in range(2):
                        k.I("pe", lambda: nc.tensor.matmul(pz[:, hf * 512:hf * 512 + 384], lhsT=lr[0:33, tsl], rhs=Gaug[dr][0][:, hf * 384:(hf + 1) * 384],
                                                           start=True, stop=True), r=[lrb, Gaug[dr][1]], w=pzb)
                    for hf in range(2):
                        k.I("act", lambda: nc.scalar.activation(out=ez[:, hf * 384:(hf + 1) * 384], in_=pz[:, hf * 512:hf * 512 + 384],
                                                                func=AF.Exp, scale=-1.0), r=pzb, w=[ezb])
                    k.I("act", lambda: nc.scalar.activation(out=lz[:], in_=ez[:], func=AF.Ln, bias=epsq[:, 3:4]), r=[ezb, epsqb], w=[lzb])
                    pbT, pbTb = ps(2)
                    for h in range(6):
                        off = (h // 4) * 512 + (h % 4) * 128
                        k.I("pe", lambda: nc.tensor.matmul(pbT[:, off:off + 128], lhsT=lz[:, h * 128:(h + 1) * 128], rhs=cums[dr],
                                                           start=True, stop=True), r=[lzb, cst_fb], w=pbTb)
                    pbt, pbtb = ps(2)
                    for hf in range(2):
                        k.I("pe", lambda: nc.tensor.matmul(pbt[:, hf * 512:hf * 512 + 384], lhsT=cums[dr], rhs=lz[:, hf * 384:(hf + 1) * 384],
                                                           start=True, stop=True), r=[lzb, cst_fb], w=pbtb)
                    for (nm, sc_) in (("Ef", 1.0), ("Em", -1.0)):
                        o_, ob2 = P[nm]
                        k.I("act", lambda: nc.scalar.activation(out=o_[:, 0:512], in_=pbT[:, 0:512], func=AF.Exp, scale=sc_), r=pbTb, w=[ob2])
                        k.I("act", lambda: nc.scalar.activation(out=o_[:, 512:768], in_=pbT[:, 512:768], func=AF.Exp, scale=sc_), r=pbTb, w=[ob2])
                    Emt, Emtb = P["Emt"]
                    for hf in range(2):
                        k.I("act", lambda: nc.scalar.activation(out=Emt[:, hf * 384:(hf + 1) * 384], in_=pbt[:, hf * 512:hf * 512 + 384],
                                                                func=AF.Exp, scale=-1.0), r=pbtb, w=[Emtb])
                    (Ef, Efb), (Em, Emb) = P["Ef"], P["Em"]
                    (qs, qsb), (ks, ksb), (kst, kstb) = P["qs"], P["ks"], P["kst"]
                    k.I("dve", lambda: nc.vector.tensor_tensor(out=qs[:].rearrange("p (h t) -> p h t", t=128), in0=qT[:, :, tsl],
                                                               in1=Ef[:].rearrange("p (h t) -> p h t", t=128), op=ALU.mult),
                        r=[qTb, Efb], w=[qsb])
                    k.I("dve", lambda: nc.vector.tensor_tensor(out=ks[:].rearrange("p (h t) -> p h t", t=128), in0=kT[:, :, tsl],
                                                               in1=Em[:].rearrange("p (h t) -> p h t", t=128), op=ALU.mult),
                        r=[kTb, Emb], w=[ksb])
                    k.I("pool", lambda: nc.gpsimd.tensor_tensor(out=kst[:], in0=kv[:, ch, 0:768], in1=Emt[:], op=ALU.mult),
                        r=[kvb, Emtb], w=[kstb])
                    if ch == chs[-1]:
                        rq.done(i)
                        rk.done(i)
                        rl.done(i)

                def main(n):
                    i, ch = items[n]
                    P = PR[n % 2]
                    kv, kvb = rv.get(i)
                    t0 = T0(i)
                    tsl = slice(ch * 128, (ch + 1) * 128)
                    (Ef, Efb) = P["Ef"]
                    (qs, qsb), (ks, ksb), (kst, kstb) = P["qs"], P["ks"], P["kst"]
                    if ch == chs[0] and dr == 1:
                        k.dma("sp", ag[:], FM[FM_OFF["a_gate"]:FM_OFF["a_gate"] + 1536, t0:t0 + 512].rearrange("(j p) t -> p j t", p=128), w=[agb])
                        k.dma("sp", of_[:], OFW[:, t0:t0 + 512].rearrange("(j p) t -> p j t", p=128), w=[ofb])
                    pA, pAb = ps(2)
                    for h in range(6):
                        off = (h // 4) * 512 + (h % 4) * 128
                        k.I("pe", lambda: nc.tensor.matmul(pA[:, off:off + 128], lhsT=ks[:, h * 128:(h + 1) * 128], rhs=qs[:, h * 128:(h + 1) * 128],
                                                           start=True, stop=True), r=[ksb, qsb], w=pAb)
                    k.I("dve", lambda: nc.vector.tensor_tensor(out=At[:, 0:512].rearrange("p (h t) -> p h t", t=128),
                                                               in0=pA[:, 0:512].rearrange("p (h t) -> p h t", t=128),
                                                               in1=masks[dr].unsqueeze(1).to_broadcast([128, 4, 128]), op=ALU.mult),
                        r=pAb + [cst_fb], w=[Atb])
                    k.I("dve", lambda: nc.vector.tensor_tensor(out=At[:, 512:768].rearrange("p (h t) -> p h t", t=128),
                                                               in0=pA[:, 512:768].rearrange("p (h t) -> p h t", t=128),
                                                               in1=masks[dr].unsqueeze(1).to_broadcast([128, 2, 128]), op=ALU.mult),
                        r=pAb + [cst_fb], w=[Atb])
                    pks = []
                    for h3 in range(3):
                        pk, pkb = ps(1)
                        pks.append((pk, pkb))
                        for hh in range(2):
                            h = h3 * 2 + hh
                            k.I("pe", lambda: nc.tensor.matmul(pk[:, hh * 256:(hh + 1) * 256], lhsT=kst[:, h * 128:(h + 1) * 128],
                                                               rhs=kv[:, ch, 768 + h * 256:768 + (h + 1) * 256], start=True, stop=True),
                                r=[kstb, kvb], w=pkb)
                        k.I("dve", lambda: nc.vector.tensor_tensor(out=tmpS[:, h3 * 2:h3 * 2 + 2, :], in0=pk.rearrange("p (a b) -> p a b", b=256),
                                                                   in1=S[:, h3 * 2:h3 * 2 + 2, :], op=ALU.add), r=pkb + [Sb], w=[tmpSb])
                    dsto = of_ if dr == 0 else ob_
                    dstob = ofb if dr == 0 else obb
                    for h3 in range(3):
                        po, pob = ps(1)
                        for hh in range(2):
                            h = h3 * 2 + hh
                            for j in range(2):
                                c = (hh * 2 + j) * 128
                                k.I("pe", lambda: nc.tensor.matmul(po[:, c:c + 128], lhsT=Sh[:, h, j * 128:(j + 1) * 128], rhs=qs[:, h * 128:(h + 1) * 128],
                                                                   start=True, stop=False), r=[Shb, qsb], w=pob)
                                k.I("pe", lambda: nc.tensor.matmul(po[:, c:c + 128], lhsT=kv[:, ch, 768 + h * 256 + j * 128:768 + h * 256 + (j + 1) * 128],
                                                                   rhs=At[:, h * 128:(h + 1) * 128], start=False, stop=True), r=[kvb, Atb], w=pob)
                        k.I("act", lambda: nc.scalar.copy(out=dsto[:, h3 * 4:(h3 + 1) * 4, tsl], in_=po.rearrange("p (a b) -> p a b", b=128)),
                            r=pob, w=[dstob])
                    ebl = Ef[:].rearrange("p (h t) -> p h t", t=128)[:, :, last:last + 1].to_broadcast([128, 6, 256])
                    k.I("dve", lambda: nc.vector.tensor_tensor(out=Sh[:], in0=tmpS[:], in1=ebl, op=ALU.mult), r=[tmpSb, Efb], w=[Shb])
                    k.I("dve", lambda: nc.vector.tensor_tensor(out=S[:], in0=tmpS[:], in1=ebl, op=ALU.mult), r=[tmpSb, Efb], w=[Sb])
                    if ch != chs[-1]:
                        return
                    rv.done(i)
                    if dr == 0:
                        k.dma("sp", OFW[:, t0:t0 + 512].rearrange("(j p) t -> p j t", p=128), of_[:], r=[ofb])
                    else:
                        k.I("pool", lambda: nc.gpsimd.tensor_tensor(out=ob_[:], in0=ob_[:], in1=of_[:], op=ALU.add), r=[obb, ofb], w=[obb])
                        k.I("act", lambda: nc.scalar.activation(out=sqh[:], in_=ob_[:], func=AF.Square), r=[obb], w=[sqhb])
                        for h in range(6):
                            pn, pnb = ps(1)
                            for j in range(2):
                                k.I("pe", lambda: nc.tensor.matmul(pn, lhsT=ones_h[:], rhs=sqh[:, h * 2 + j, :], start=(j == 0), stop=(j == 1)),
                                    r=[ones_hb, sqhb], w=pnb)
                            k.I("act", lambda: nc.scalar.activation(out=rr[:], in_=pn, func=AF.Ln, bias=epsq[:, 2:3], scale=1.0 / 256), r=pnb + [epsqb], w=[rrb])
                            k.I("act", lambda: nc.scalar.activation(out=rr[:], in_=rr[:], func=AF.Exp, scale=-0.5), r=[rrb], w=[rrb])
                            for j in range(2):
                                k.I("dve", lambda: nc.vector.scalar_tensor_tensor(out=ob_[:, h * 2 + j, :], in0=ob_[:, h * 2 + j, :], scalar=gng[:, j:j + 1],
                                                                                  in1=rr[:], op0=ALU.mult, op1=ALU.mult), r=[obb, gngb, rrb], w=[obb])
                        k.I("pool", lambda: nc.gpsimd.tensor_tensor(out=sqh[:], in0=ob_[:], in1=ag[:], op=ALU.mult), r=[obb, agb], w=[sqhb])
                        k.dma("sp", OFM[0:1536, t0:t0 + 512].rearrange("(j p) t -> p j t", p=128), sqh[:], r=[sqhb])

                nit = len(items)
                prep(0)
                for n in range(nit):
                    if n + 1 < nit:
                        prep(n + 1)
                    main(n)
                if dr == 0:
                    k.reset()

    with contextlib.ExitStack() as st:
        wraw, wrawb = sb("s_wraw", [128, 8, 128], F32, st)
        WT, WTb = sb("s_WT", [128, 8, 128], BF16, st)
        Bg, Bgb = sb("s_Bg", [128, 8, 128], F32, st)
        k.dma("sp", wraw[:], T["sgu_w"][l].rearrange("g p q -> p g q"), w=[wrawb])
        k.dma("sp", Bg[:].rearrange("p g q -> p (g q)"), T["sgu_b"][l:l + 1].rearrange("o g q -> o (g q)").to_broadcast([128, 1024]), w=[Bgb])
        for g in range(8):
            pa, pb = ps(1)
            k.I("pe", lambda: nc.tensor.matmul(pa[:, 0:128], lhsT=wraw[:, g, :], rhs=ident_f, start=True, stop=True), r=[wrawb, cst_fb], w=pb)
            k.I("act", lambda: nc.scalar.copy(out=WT[:, g, :], in_=pa[:, 0:128]), r=pb, w=[WTb])
        vv, vvb = sb("s_vv", [128, 4, 1024], BF16, st)
        uT, uTb = sb("s_u", [128, 8, 512], BF16, st)
        gT, gTb = sb("s_g", [128, 8, 512], BF16, st)
        tt, ttb = sb("s_t", [128, 512], F32, st)
        ug, ugb = sb("s_ug", [128, 8, 512], F32, st)
        obt, obtb = sb("s_ob", [128, 8, 512], BF16, st)
        sgu_tiles = [t_ for (t_, T_, md_) in L.get("p1", [(x_, 512, "full") for x_ in range(0, L["NT"], 512)]) if md_ == "full"]

        def sgu_gen():
            for t0 in sgu_tiles:
                k.dma("sp", vv[:], TM[t0:t0 + 512, TM_OFF["b_v"]:TM_OFF["b_v"] + 1024].rearrange("(c p) f -> p c f", p=128), w=[vvb])
                k.dma("sp", uT[:], FM[FM_OFF["b_u"]:FM_OFF["b_u"] + 1024, t0:t0 + 512].rearrange("(g p) t -> p g t", p=128), w=[uTb])
                k.dma("sp", gT[:], FM[FM_OFF["b_gate"]:FM_OFF["b_gate"] + 1024, t0:t0 + 512].rearrange("(g p) t -> p g t", p=128), w=[gTb])
                k.I("pool", lambda: nc.gpsimd.tensor_tensor(out=ug[:], in0=uT[:], in1=gT[:], op=ALU.mult), r=[uTb, gTb], w=[ugb])
                yield
                for g in range(8):
                    pa, pb = ps(1)
                    for c in range(4):
                        k.I("pe", lambda: nc.tensor.matmul(pa[:, c * 128:(c + 1) * 128], lhsT=vv[:, c, g * 128:(g + 1) * 128], rhs=WT[:, g, :],
                                                           start=True, stop=True), r=[vvb, WTb], w=pb)
                    k.I("dve", lambda: nc.vector.tensor_tensor(out=tt[:].rearrange("p (c q) -> p c q", q=128), in0=pa.rearrange("p (c q) -> p c q", q=128),
                                                               in1=Bg[:, g:g + 1, :].to_broadcast([128, 4, 128]), op=ALU.add), r=pb + [Bgb], w=[ttb])
                    k.I("dve", lambda: nc.vector.tensor_tensor(out=obt[:, g, :], in0=tt[:], in1=ug[:, g, :], op=ALU.mult), r=[ttb, ugb], w=[obtb])
                    yield
                k.dma("sp", OFM[1536:2560, t0:t0 + 512].rearrange("(g p) t -> p g t", p=128), obt[:], r=[obtb])
                yield

        sg_ = sgu_gen()

        def sgu_step():
            try:
                next(sg_)
                return True
            except StopIteration:
                return False

        NB = L["NT"] // 128
        kvm, kvmb = sb("a_kvm", [128, NB], F32, st)
        k.dma("sp", kvm[:], T["kvalid"][l, :, 0:NB], w=[kvmb])
        sk, skb = sb("a_sk", [128, 12], F32, st)
        k.dma("sp", sk[:], T["sink"][l:l + 1, :].to_broadcast([128, 12]), w=[skb])
        k.I("act", lambda: nc.scalar.activation(out=sk[:], in_=sk[:], func=AF.Exp), r=[skb], w=[skb])
        qTs = [sb("a_qT%d" % i, [128, 12, 512], BF16, st) for i in range(2)]
        kTs = [sb("a_kT%d" % i, [128, 4, 768], BF16, st) for i in range(2)]
        vts = [sb("a_v%d" % i, [128, 6, 512], BF16, st) for i in range(2)]
        cgs = [sb("a_cg%d" % i, [128, 12, 512], BF16, st) for i in range(2)]
        ocs = [sb("a_oc%d" % i, [128, 12, 512], BF16, st) for i in range(2)]
        Eb = [sb("a_E%d" % i, [128, 3, 384], BF16, st) for i in range(3)]
        rden, rdenb = sb("a_rd", [128, 384], F32, st)
        o1, o1b = sb("a_o1", [128, 384], F32, st)
        scale = float(128.0 ** -0.5)
        tiles = []
        for (s0, slen) in L["segs"]:
            for sc in range(slen // 512):
                tiles.append((s0 + sc * 512, s0, slen))

        def krange(i):
            t0, s0, slen = tiles[i]
            lo = max(t0 - 128, s0)
            hi = min(t0 + 640, s0 + slen)
            return lo, hi, lo - (t0 - 128)

        def kload(i, t, b):
            lo, hi, ko = krange(i)
            k.dma("sp", t[:, :, ko:ko + hi - lo], FM[FM_OFF["c_k"]:FM_OFF["c_k"] + 512, lo:hi].rearrange("(h p) t -> p h t", p=128), w=[b])

        def vload(i, t, b):
            lo, hi, ko = krange(i)
            k.dma("sp", t[:, ko // 128:(ko + hi - lo) // 128, :], TM[lo:hi, TM_OFF["c_v"]:TM_OFF["c_v"] + 512].rearrange("(c p) f -> p c f", p=128), w=[b])

        rq = Ring(qTs, lambda i, t, b: k.dma("sp", t[:], FM[FM_OFF["c_q"]:FM_OFF["c_q"] + 1536, tiles[i][0]:tiles[i][0] + 512].rearrange("(h p) t -> p h t", p=128), w=[b]), len(tiles))
        rk = Ring(kTs, kload, len(tiles))
        rv = Ring(vts, vload, len(tiles))
        rg = Ring(cgs, lambda i, t, b: k.dma("sp", t[:], FM[FM_OFF["c_gate"]:FM_OFF["c_gate"] + 1536, tiles[i][0]:tiles[i][0] + 512].rearrange("(h p) t -> p h t", p=128), w=[b]), len(tiles))
        items = [(i, blk, hk) for i in range(len(tiles)) for blk in range(4) for hk in range(4)]

        def jlist(i, blk):
            t0, s0, slen = tiles[i]
            q0 = t0 + blk * 128
            return [j for j in (-1, 0, 1) if s0 <= q0 + j * 128 < s0 + slen]

        def stageA(n):
            i, blk, hk = items[n]
            t0 = tiles[i][0]
            q0 = t0 + blk * 128
            qT, qTb = rq.get(i)
            kT, kTb = rk.get(i)
            Et, Etb = Eb[n % 3]
            qv = qT[:, hk * 3:(hk + 1) * 3, blk * 128:(blk + 1) * 128]
            for j in jlist(i, blk):
                kb = blk + 1 + j
                pa, pb = ps(1)
                k.I("pe", lambda: nc.tensor.matmul(pa[:, 0:384].rearrange("p (g t) -> p g t", t=128), lhsT=kT[:, hk, kb * 128:(kb + 1) * 128], rhs=qv,
                                                   start=True, stop=True), r=[kTb, qTb], w=pb)
                k.I("act", lambda: nc.scalar.activation(out=Et[:, j + 1, :], in_=pa[:, 0:384], func=AF.Exp, scale=scale), r=pb, w=[Etb])
                if j != 0:
                    mk = masks_h[1] if j == -1 else masks_h[0]
                    gblk = (q0 + j * 128) // 128
                    k.I("dve", lambda: nc.vector.scalar_tensor_tensor(out=Et[:, j + 1, :].rearrange("p (g t) -> p g t", t=128),
                                                                      in0=Et[:, j + 1, :].rearrange("p (g t) -> p g t", t=128),
                                                                      scalar=kvm[:, gblk:gblk + 1],
                                                                      in1=mk.unsqueeze(1).to_broadcast([128, 3, 128]),
                                                                      op0=ALU.mult, op1=ALU.mult), r=[Etb, kvmb, cst_hb], w=[Etb])
            if blk == 3 and hk == 3:
                rq.done(i)
                rk.done(i)

        def stageB(n):
            i, blk, hk = items[n]
            t0 = tiles[i][0]
            vt, vtb = rv.get(i)
            cg, cgb = rg.get(i)
            oc, ocb = ocs[i % 2]
            Et, Etb = Eb[n % 3]
            js = jlist(i, blk)
            po, pob = ps(1)
            pd, pdb = ps(1)
            for ji, j in enumerate(js):
                kb = blk + 1 + j
                k.I("pe", lambda: nc.tensor.matmul(po[:, 0:384], lhsT=vt[:, kb, hk * 128:(hk + 1) * 128], rhs=Et[:, j + 1, :],
                                                   start=(ji == 0), stop=(ji == len(js) - 1)), r=[vtb, Etb], w=pob)
            for ji, j in enumerate(js):
                k.I("pe", lambda: nc.tensor.matmul(pd[:, 0:384], lhsT=ones_h[:], rhs=Et[:, j + 1, :],
                                                   start=(ji == 0), stop=(ji == len(js) - 1)), r=[ones_hb, Etb], w=pdb)
            k.I("dve", lambda: nc.vector.tensor_tensor(out=rden[:].rearrange("p (g t) -> p g t", t=128), in0=pd[:, 0:384].rearrange("p (g t) -> p g t", t=128),
                                                       in1=sk[:, hk * 3:(hk + 1) * 3].unsqueeze(2).to_broadcast([128, 3, 128]), op=ALU.add),
                r=pdb + [skb], w=[rdenb])
            k.I("act", lambda: nc.scalar.activation(out=rden[:], in_=rden[:], func=AF.Ln), r=[rdenb], w=[rdenb])
            k.I("act", lambda: nc.scalar.activation(out=rden[:], in_=rden[:], func=AF.Exp, scale=-1.0), r=[rdenb], w=[rdenb])
            k.I("dve", lambda: nc.vector.tensor_tensor(out=o1[:], in0=po[:, 0:384], in1=rden[:], op=ALU.mult), r=pob + [rdenb], w=[o1b])
            k.I("pool", lambda: nc.gpsimd.tensor_tensor(out=oc[:, hk * 3:(hk + 1) * 3, blk * 128:(blk + 1) * 128],
                                                        in0=o1[:].rearrange("p (g t) -> p g t", t=128),
                                                        in1=cg[:, hk * 3:(hk + 1) * 3, blk * 128:(blk + 1) * 128], op=ALU.mult),
                r=[o1b, cgb], w=[ocb])
            if blk == 3 and hk == 3:
                rv.done(i)
                rg.done(i)
                k.dma("sp", OFM[2560:4096, t0:t0 + 512].rearrange("(h p) t -> p h t", p=128), oc[:], r=[ocb])

        nit = len(items)
        stageA(0)
        for n in range(nit):
            if n + 1 < nit:
                stageA(n + 1)
            stageB(n)
            if n % 2 == 1:
                sgu_step()
        while sgu_step():
            pass

def make_consts():
    s = np.arange(128)[:, None]
    t = np.arange(128)[None, :]
    ident = np.eye(128, dtype=np.float32)
    U = (s <= t).astype(np.float32)
    Lm = (s >= t).astype(np.float32)
    rot = np.zeros((128, 128), np.float32)
    for d in range(16):
        rot[16 + d, d] = 1.0
        rot[d, 16 + d] = 1.0
    return np.concatenate([ident, U, Lm, -U / 16.0, -Lm / 16.0, rot], axis=1).astype(np.float32)


def rope_tables(pos):
    inv = ROPE_THETA ** (-np.arange(0, 32, 2, dtype=np.float32) / 32.0)
    ang = pos.astype(np.float32)[None, :] * inv[:, None].astype(np.float32)
    c = np.cos(ang).astype(np.float32)
    s_ = np.sin(ang).astype(np.float32)
    return np.concatenate([c, c], 0), np.concatenate([-s_, s_], 0)


def full_cfg():
    H = HALO
    n0 = (2048 + 2 * H) // 512
    l0 = dict(NT=2048 + 2048 + 4 * H, segs=[(0, 2048), (2048, 2048 + 4 * H)], xsrc="x0",
              p1=[(i * 512, 512, "full") for i in range(4)] + [(2048, H, "halo")] +
                 [(2048 + H + i * 512, 512, "full") for i in range(n0)] + [(2048 + H + n0 * 512, H, "halo")],
              p3=[(i * 512, "x1", i * 512) for i in range(4)] +
                 [(2048 + H + i * 512, "x1", 2048 + i * 512) for i in range(n0)])
    l1 = dict(NT=2048 + 2048 + 2 * H, segs=[(0, 2048), (2048, 2048 + 2 * H)], xsrc="x1",
              p1=[(i * 512, 512, "full") for i in range(4)] + [(2048, H, "halo")] +
                 [(2048 + H + i * 512, 512, "full") for i in range(4)] + [(2048 + H + 2048, H, "halo")],
              p3=[(i * 512, "yp", i * 512) for i in range(4)] +
                 [(2048 + H + i * 512, "ys", i * 512) for i in range(4)])
    return dict(depth=2, layers=[l0, l1], outs=[("yp", 2048), ("ys", 2048)])


_CACHE = {}


def kernel(x_prompt, x_sample, norm_gain, w_in, gla_gate_up, gla_gate_bias, gla_norm_gain,
           sgu_ln_gain, sgu_ln_bias, sgu_w, sgu_b, q_norm_gain, k_norm_gain, sink,
           gate_bias, w_br, w_out):
    H = HALO
    cfg = full_cfg()
    f32 = lambda a: np.ascontiguousarray(np.asarray(a, dtype=np.float32))
    x_prompt, x_sample = f32(x_prompt), f32(x_sample)
    shared = dict(norm_gain=f32(norm_gain), w_in=f32(w_in), gla_gate_up=f32(gla_gate_up), gla_gate_bias=f32(gla_gate_bias),
                  gla_norm_gain=f32(gla_norm_gain), sgu_ln_gain=f32(sgu_ln_gain), sgu_ln_bias=f32(sgu_ln_bias),
                  sgu_w=f32(sgu_w), sgu_b=f32(sgu_b), q_norm_gain=f32(q_norm_gain), k_norm_gain=f32(k_norm_gain),
                  sink=f32(sink), gate_bias=f32(gate_bias), w_br=f32(w_br), w_out=f32(w_out), cst=make_consts())
    NT0 = cfg["layers"][0]["NT"]
    NBLK = NT0 // 128
    S = x_sample.shape[1]
    in_maps = []
    for c in range(8):
        x0 = np.zeros((NT0, D), np.float32)
        x0[0:2048] = x_prompt[c]
        g0 = 2048 * c - 2 * H
        lo, hi = max(g0, 0), min(g0 + 2048 + 4 * H, S)
        x0[2048 + (lo - g0):2048 + (hi - g0)] = x_sample[0, lo:hi]
        rc = np.zeros((2, 32, NT0), np.float32)
        rs = np.zeros((2, 32, NT0), np.float32)
        kvd = np.zeros((2, 128, NBLK), np.float32)
        pos0 = np.concatenate([np.arange(2048), g0 + np.arange(2048 + 4 * H)])
        val0 = np.concatenate([np.ones(2048), ((g0 + np.arange(2048 + 4 * H) >= 0) & (g0 + np.arange(2048 + 4 * H) < S))])
        g1 = 2048 * c - H
        pos1 = np.concatenate([np.arange(2048), g1 + np.arange(2048 + 2 * H)])
        val1 = np.concatenate([np.ones(2048), ((g1 + np.arange(2048 + 2 * H) >= 0) & (g1 + np.arange(2048 + 2 * H) < S))])
        for li, (pos, val) in enumerate(((pos0, val0), (pos1, val1))):
            cc, ss = rope_tables(pos)
            rc[li, :, :len(pos)] = cc
            rs[li, :, :len(pos)] = ss
            kvd[li, :, :len(pos) // 128] = val.reshape(-1, 128)[:, 0][None, :]
        m = dict(shared)
        m.update(x0=x0, ropec=rc, ropes=rs, kvalid=kvd)
        in_maps.append(m)
    if "nc" not in _CACHE:
        _CACHE["nc"] = build_program(cfg)
    res = run_bass_kernel_spmd(_CACHE["nc"], in_maps, core_ids=list(range(8)))
    yp = np.stack([res.results[c]["yp"] for c in range(8)], 0).astype(np.float32)
    ys = np.concatenate([res.results[c]["ys"] for c in range(8)], 0)[None].astype(np.float32)
    return yp, ys
```
